# Optimizing a Trainium2 kernel written in Bass

```python
import jax, jax.numpy as jnp
from jax import lax
import numpy as np

D_MODEL = 1024
BATCH = 2
SEQ = 8192
DEPTH = 1

N_META = 16
GRID_W = 64
EXPAND = 2
D_MIX = EXPAND * D_MODEL
GLA_WIDTH = D_MIX // 2
GLA_HEADS = 4
GLA_DV = GLA_WIDTH // GLA_HEADS
GLA_DK = GLA_DV // 2
GLA_KEY_WIDTH = GLA_HEADS * GLA_DK
GLA_GATE_RANK = 16
GLA_TAU = 16.0
GLA_CHUNK = 64
NA_WIDTH = D_MIX - GLA_WIDTH
NA_HEAD_DIM = 64
NA_HEADS = NA_WIDTH // NA_HEAD_DIM
NA_WIN_H_MAX = 8
NA_WIN_W = 16
RMS_EPS = 1e-6

IN_SPLIT_WIDTHS = (GLA_KEY_WIDTH, GLA_KEY_WIDTH, GLA_WIDTH, GLA_GATE_RANK, GLA_GATE_RANK, GLA_WIDTH,
                   NA_WIDTH, NA_WIDTH, NA_WIDTH, NA_WIDTH)
D_IN_PROJ = sum(IN_SPLIT_WIDTHS)

kernel_name = "hybrid_gla_natten_meta_encoder"


def _rmsnorm(x, g):
    xf = x.astype(jnp.float32)
    y = xf * lax.rsqrt(jnp.mean(xf * xf, axis=-1, keepdims=True) + RMS_EPS)
    return (y * g.astype(jnp.float32)).astype(x.dtype)


def _gla_chunked(q, k, v, g):
    B, T, H, DK = q.shape
    DV = v.shape[-1]
    C = GLA_CHUNK
    N = T // C
    q = q.reshape(B, N, C, H, DK)
    k = k.reshape(B, N, C, H, DK)
    v = v.reshape(B, N, C, H, DV)
    G = jnp.cumsum(g.astype(jnp.float32).reshape(B, N, C, H, DK), axis=2)
    G_last = G[:, :, -1:]
    q_in = q * jnp.exp(G)
    k_in = k * jnp.exp(-G)
    k_st = k * jnp.exp(G_last - G)
    mask = jnp.tril(jnp.ones((C, C), dtype=bool))
    A = jnp.einsum('bnihd,bnjhd->bnhij', q_in, k_in)
    A = jnp.where(mask, A, 0.0)
    o_intra = jnp.einsum('bnhij,bnjhv->bnihv', A, v)
    U = jnp.einsum('bnjhd,bnjhv->bnhdv', k_st, v)
    a = jnp.exp(G_last[:, :, 0])

    def step(S, inp):
        a_n, U_n = inp
        return a_n[..., None] * S + U_n, S

    S0 = jnp.zeros((B, H, DK, DV), dtype=U.dtype)
    _, S_prev = lax.scan(step, S0, (jnp.moveaxis(a, 1, 0), jnp.moveaxis(U, 1, 0)))
    S_prev = jnp.moveaxis(S_prev, 0, 1)
    o_inter = jnp.einsum('bnihd,bnhdv->bnihv', q_in, S_prev)
    return (o_intra + o_inter).reshape(B, T, H, DV).astype(v.dtype)


def _neighbourhood_attention(q, k, v, qm, km, vm, rpb, meta_bias):
    B, R, W, H, D = q.shape
    wh = min(NA_WIN_H_MAX, R)
    ww = NA_WIN_W
    scale = D ** -0.5
    cols = jnp.arange(W)
    cs = jnp.clip(cols - ww // 2, 0, W - ww)
    col_idx = cs[:, None] + jnp.arange(ww)[None, :]
    col_bias_idx = col_idx - cols[:, None] + (NA_WIN_W - 1)
    mb = meta_bias[None, :, None, :]

    def row_block(r):
        rs = jnp.clip(r - wh // 2, 0, R - wh)
        q_r = lax.dynamic_index_in_dim(q, r, axis=1, keepdims=False)
        k_blk = lax.dynamic_slice_in_dim(k, rs, wh, axis=1)
        v_blk = lax.dynamic_slice_in_dim(v, rs, wh, axis=1)
        k_g = k_blk[:, :, col_idx]
        v_g = v_blk[:, :, col_idx]
        s_loc = jnp.einsum('bchd,bicjhd->bhcij', q_r, k_g) * scale
        row_bias_idx = rs + jnp.arange(wh) - r + (NA_WIN_H_MAX - 1)
        bias = rpb[:, row_bias_idx][:, :, col_bias_idx]
        s_loc = s_loc + jnp.transpose(bias, (0, 2, 1, 3))[None]
        s_meta = jnp.einsum('bchd,bmhd->bhcm', q_r, km) * scale + mb
        s = jnp.concatenate([s_loc.reshape(B, H, W, wh * ww), s_meta], axis=-1)
        p = jax.nn.softmax(s.astype(jnp.float32), axis=-1).astype(v.dtype)
        p_loc = p[..., :wh * ww].reshape(B, H, W, wh, ww)
        p_meta = p[..., wh * ww:]
        return (jnp.einsum('bhcij,bicjhd->bchd', p_loc, v_g)
                + jnp.einsum('bhcm,bmhd->bchd', p_meta, vm))

    o = lax.map(row_block, jnp.arange(R))
    o = jnp.moveaxis(o, 0, 1)
    s_mm = jnp.einsum('bqhd,bmhd->bhqm', qm, km) * scale + mb
    p_mm = jax.nn.softmax(s_mm.astype(jnp.float32), axis=-1).astype(vm.dtype)
    om = jnp.einsum('bhqm,bmhd->bqhd', p_mm, vm)
    return o, om


def setup_inputs(seed: int = 0) -> dict:
    key = jax.random.key(seed)
    ks = jax.random.split(key, 16)
    f32 = jnp.float32
    nrm = lambda k, s, sc: jax.random.normal(k, s, f32) * sc
    return {
        "x": nrm(ks[0], (BATCH, SEQ, D_MODEL), 1.0),
        "meta_tokens": nrm(ks[1], (N_META, D_MODEL), 1.0),
        "norm_g": 1.0 + nrm(ks[2], (DEPTH, D_MODEL), 0.02),
        "w_in": nrm(ks[3], (DEPTH, D_MODEL, D_IN_PROJ), D_MODEL ** -0.5),
        "w_decay_fwd": nrm(ks[4], (DEPTH, GLA_GATE_RANK, GLA_KEY_WIDTH), GLA_GATE_RANK ** -0.5),
        "b_decay_fwd": nrm(ks[5], (DEPTH, GLA_KEY_WIDTH), 0.1),
        "w_decay_bwd": nrm(ks[6], (DEPTH, GLA_GATE_RANK, GLA_KEY_WIDTH), GLA_GATE_RANK ** -0.5),
        "b_decay_bwd": nrm(ks[7], (DEPTH, GLA_KEY_WIDTH), 0.1),
        "gla_out_norm_g": 1.0 + nrm(ks[8], (DEPTH, GLA_DV), 0.02),
        "q_norm_g": 1.0 + nrm(ks[9], (DEPTH, NA_HEAD_DIM), 0.02),
        "k_norm_g": 1.0 + nrm(ks[10], (DEPTH, NA_HEAD_DIM), 0.02),
        "rpb": nrm(ks[11], (DEPTH, NA_HEADS, 2 * NA_WIN_H_MAX - 1, 2 * NA_WIN_W - 1), 0.02),
        "meta_bias": nrm(ks[12], (DEPTH, NA_HEADS, N_META), 0.02),
        "w_out": nrm(ks[13], (DEPTH, D_MIX, D_MODEL), D_MIX ** -0.5),
    }


def reference(x, meta_tokens, norm_g, w_in, w_decay_fwd, b_decay_fwd, w_decay_bwd, b_decay_bwd,
              gla_out_norm_g, q_norm_g, k_norm_g, rpb, meta_bias, w_out):
    B, S, _ = x.shape
    R = S // GRID_W
    L = N_META + S
    pad = GLA_CHUNK - N_META
    split_at = [int(v) for v in np.cumsum(IN_SPLIT_WIDTHS)[:-1]]
    h = jnp.concatenate([jnp.broadcast_to(meta_tokens[None], (B, N_META, D_MODEL)).astype(x.dtype), x], axis=1)

    def pad_front(t):
        return jnp.pad(t, ((0, 0), (pad, 0), (0, 0), (0, 0)))

    def flip(t):
        return jnp.flip(t, axis=1)

    for l in range(DEPTH):
        u = _rmsnorm(h, norm_g[l])
        z = u @ w_in[l]
        gq, gk, gv, r_f, r_b, g_gate, nq, nk, nv, n_gate = jnp.split(z, split_at, axis=-1)

        q = (gq * GLA_DK ** -0.5).reshape(B, L, GLA_HEADS, GLA_DK)
        k = gk.reshape(B, L, GLA_HEADS, GLA_DK)
        v = gv.reshape(B, L, GLA_HEADS, GLA_DV)
        lg_f = (jax.nn.log_sigmoid(r_f @ w_decay_fwd[l] + b_decay_fwd[l]) / GLA_TAU).reshape(B, L, GLA_HEADS, GLA_DK)
        lg_b = (jax.nn.log_sigmoid(r_b @ w_decay_bwd[l] + b_decay_bwd[l]) / GLA_TAU).reshape(B, L, GLA_HEADS, GLA_DK)
        qp, kp, vp = pad_front(q), pad_front(k), pad_front(v)
        o_f = _gla_chunked(qp, kp, vp, pad_front(lg_f))
        o_b = flip(_gla_chunked(flip(qp), flip(kp), flip(vp), flip(pad_front(lg_b))))
        o_gla = _rmsnorm((o_f + o_b)[:, pad:], gla_out_norm_g[l]).reshape(B, L, GLA_WIDTH)
        o_gla = o_gla * jax.nn.silu(g_gate)

        qn = _rmsnorm(nq.reshape(B, L, NA_HEADS, NA_HEAD_DIM), q_norm_g[l])
        kn = _rmsnorm(nk.reshape(B, L, NA_HEADS, NA_HEAD_DIM), k_norm_g[l])
        vn = nv.reshape(B, L, NA_HEADS, NA_HEAD_DIM)
        grid = lambda t: t[:, N_META:].reshape(B, R, GRID_W, NA_HEADS, NA_HEAD_DIM)
        o_grid, o_meta = _neighbourhood_attention(grid(qn), grid(kn), grid(vn),
                                                  qn[:, :N_META], kn[:, :N_META], vn[:, :N_META],
                                                  rpb[l], meta_bias[l])
        o_na = jnp.concatenate([o_meta, o_grid.reshape(B, S, NA_HEADS, NA_HEAD_DIM)], axis=1).reshape(B, L, NA_WIDTH)
        o_na = o_na * jax.nn.silu(n_gate)

        h = h + jnp.concatenate([o_gla, o_na], axis=-1) @ w_out[l]

    return h[:, N_META:]
```

```python
import numpy as np
from contextlib import ExitStack
import concourse.bass as bass
import concourse.mybir as mybir
from concourse.bass_utils import run_bass_kernel_spmd

F32 = mybir.dt.float32
BF16 = mybir.dt.bfloat16
AF = mybir.ActivationFunctionType
ALU = mybir.AluOpType

D_MODEL = 1024
SEQ = 8192
N_META = 16
L_TOK = SEQ + N_META
GRID_W = 64
N_ROWS = SEQ // GRID_W
EPS = 1e-6
NFM = 800
NTM = 1152
NCOL = NFM + NTM
MASKV = -1.0e4
NTILE = 64
N_BT = 84


class Sched:
    ENGS = ("pe", "act", "dve", "pool", "sp")
    NDMA = {"sp": 32, "pool": 16, "act": 16}

    def __init__(self, nc, stack):
        self.nc = nc
        self.prog = {e: [] for e in self.ENGS}
        self.esem = {e: stack.enter_context(nc.semaphore("s_" + e)) for e in self.ENGS}
        self.dsem = {q: [stack.enter_context(nc.semaphore("d_%s%d" % (q, i))) for i in range(n)]
                     for q, n in self.NDMA.items()}
        self.ccsem = stack.enter_context(nc.semaphore("s_cc"))
        self.cccnt = 0
        self.cclast = None
        self.cnt = {e: 0 for e in self.ENGS}
        self.dcnt = {q: 0 for q in self.NDMA}
        self.dlast = {}
        self.known = {e: {} for e in self.ENGS}
        self.buf = {}
        self.nwait = 0

    def _wait(self, eng, tok):
        if tok is None:
            return
        sem, val = tok
        k = self.known[eng]
        if k.get(id(sem), 0) >= val:
            return
        k[id(sem)] = val
        self.prog[eng].append(("w", sem, val))
        self.nwait += 1

    def _deps(self, eng, reads, writes):
        for key in reads:
            st = self.buf.get(key)
            if st is not None:
                self._waitdep(eng, st[0])
        for key in writes:
            st = self.buf.get(key)
            if st is not None:
                self._waitdep(eng, st[0])
                for t in st[1]:
                    self._waitdep(eng, t)

    def _waitdep(self, eng, tok):
        if tok is None:
            return
        if eng == "pe" and tok[0] is self.esem["pe"]:
            return
        self._wait(eng, tok)

    def _mark(self, tok, reads, writes):
        for key in reads:
            st = self.buf.setdefault(key, [None, []])
            st[1].append(tok)
        for key in writes:
            self.buf[key] = [tok, []]

    @staticmethod
    def _split(r, w):
        r2 = [k for k in r if not (isinstance(k, tuple) and k[0] == "ps")]
        w2 = list(w) + [k for k in r if isinstance(k, tuple) and k[0] == "ps"]
        return r2, w2

    def op(self, eng, name, kw, r=(), w=(), inc=True):
        r, w = self._split(r, w)
        self._deps(eng, r, w)
        if inc:
            self.cnt[eng] += 1
            tok = (self.esem[eng], self.cnt[eng])
            self.prog[eng].append(("o", name, kw, self.esem[eng], 1))
        else:
            assert eng == "pe"
            tok = (self.esem[eng], self.cnt[eng] + 1)
            self.prog[eng].append(("o", name, kw, None, 0))
        self._mark(tok, r, w)
        return tok

    def dma(self, q, out, in_, r=(), w=()):
        self._deps(q, r, w)
        n = self.NDMA[q]
        k = self.dcnt[q]
        self.dcnt[q] += 1
        sem = self.dsem[q][k % n]
        if k >= n:
            self._wait(q, (sem, 16 * (k // n)))
        tok = (sem, 16 * (k // n + 1))
        self.prog[q].append(("o", "dma_start", dict(out=out, in_=in_), sem, 16))
        self.dlast[id(sem)] = tok
        self._mark(tok, r, w)
        return tok

    def collective(self, kw, r=(), w=()):
        self._deps("pool", list(r), list(w))
        self.cccnt += 1
        tok = (self.ccsem, self.cccnt)
        self.prog["pool"].append(("o", "collective_compute", kw, self.ccsem, 1))
        self.cclast = tok
        self._mark(tok, list(r), list(w))
        return tok

    def barrier(self, with_cc=False):
        toks = [(self.esem[e], self.cnt[e]) for e in self.ENGS if self.cnt[e] > 0]
        toks += list(self.dlast.values())
        if with_cc and self.cclast is not None:
            toks.append(self.cclast)
        for e in self.ENGS:
            for t in toks:
                self._wait(e, t)
        self.buf = {}

    def emit(self, eng, e):
        for item in self.prog[eng]:
            if item[0] == "w":
                e.wait_ge(item[1], item[2])
            else:
                _, name, kw, sem, inc = item
                ins = getattr(e, name)(**kw)
                if sem is not None:
                    ins.then_inc(sem, inc)


class Arena:
    def __init__(self, t32, nbytes):
        self.t32 = t32
        self.tb = t32.bitcast(BF16)
        self.nbytes = nbytes
        self.off = 0
        self.peak = 0

    def alloc(self, free_shape, dt):
        esz = 4 if dt == F32 else 2
        n = int(np.prod(free_shape))
        self.off = (self.off + 63) // 64 * 64
        o = self.off
        self.off += n * esz
        self.peak = max(self.peak, self.off)
        assert self.off <= self.nbytes, ("SBUF arena overflow", self.off, self.nbytes)
        base = self.t32 if dt == F32 else self.tb
        ap = base[:, o // esz: o // esz + n]
        if len(free_shape) == 2:
            ap = ap.rearrange("p (a b) -> p a b", a=free_shape[0])
        elif len(free_shape) == 3:
            ap = ap.rearrange("p (a b c) -> p a b c", a=free_shape[0], b=free_shape[1])
        return ap


def build_program(seq=8192, debug=0):
    nc = bass.Bass("TRN2", target_bir_lowering=False)
    L = seq + N_META
    ntile = seq // 128
    nrows = seq // GRID_W

    def din(name, shape):
        return nc.dram_tensor(name, list(shape), F32, kind="ExternalInput")

    x_d = din("x", [seq, D_MODEL])
    meta_d = din("meta", [N_META, D_MODEL])
    gt_d = din("gtile", [128, D_MODEL])
    win_d = din("w_in", [D_MODEL, NCOL])
    wdec_d = din("wdec", [17, 256])
    gout_d = din("gout", [128, 256])
    qkg_d = din("qkg", [128, 2])
    bias_d = din("biast", [N_BT, 128, 128])
    mb_d = din("metab", [N_META, 4])
    wout_d = din("w_out", [2048, 256])
    cst_d = din("consts", [8, 128, 128])
    xres_d = din("xres", [256, seq])
    out_d = nc.dram_tensor("out", [256, seq], F32, kind="ExternalOutput")

    st_qk = nc.dram_tensor("st_qk", [128, 2, seq], BF16)
    st_rT = nc.dram_tensor("st_rT", [32, L], F32)
    st_nqT = nc.dram_tensor("st_nqT", [256, seq], BF16)
    st_nkT = nc.dram_tensor("st_nkT", [256, L], BF16)
    st_tm = nc.dram_tensor("st_tm", [L, NTM], BF16)
    XC = 512
    nxc = seq // XC
    exg_in_l = [nc.dram_tensor("exg_in%d" % k, [256, XC], BF16) for k in range(nxc)]
    exn_in_l = [nc.dram_tensor("exn_in%d" % k, [256, XC], BF16) for k in range(nxc)]
    exg_out_l = [nc.dram_tensor("exg_out%d" % k, [1024, XC], BF16) for k in range(nxc)]
    exn_out_l = [nc.dram_tensor("exn_out%d" % k, [1024, XC], BF16) for k in range(nxc)]
    ccg, ccn = {}, {}
    RG = [[0, 1, 2, 3], [4, 5, 6, 7]]

    def ex_in_ap(r0, r1, t0, n):
        k, o = t0 // XC, t0 % XC
        assert (r0, r1) in ((0, 256), (256, 512))
        return (exg_in_l if r0 == 0 else exn_in_l)[k][0:256, o:o + n]
    if debug:
        dbg_ex = nc.dram_tensor("dbg_ex", [512, seq], BF16, kind="ExternalOutput")

    ARENA_BYTES = 190 * 1024
    with ExitStack() as stack:
        arena_t = stack.enter_context(nc.sbuf_tensor("arena", [128, ARENA_BYTES // 4], F32))
        banks = [stack.enter_context(nc.psum_tensor("bank%d" % i, [128, 512], F32)) for i in range(8)]
        banksb = [b.bitcast(BF16) for b in banks]
        S = Sched(nc, stack)
        S_real = S
        A = Arena(arena_t, ARENA_BYTES)

        class Rec:
            def __init__(self):
                self.l = []

            def op(self, *a, **k):
                self.l.append(("op", a, k))

            def dma(self, *a, **k):
                self.l.append(("dma", a, k))

        def interleave(thunks):
            nonlocal S
            lists = []
            for th in thunks:
                rec = Rec()
                S = rec
                try:
                    th()
                finally:
                    S = S_real
                if rec.l:
                    lists.append(rec.l)
            pos = [0] * len(lists)
            live = True
            while live:
                live = False
                for li, l in enumerate(lists):
                    if pos[li] < len(l):
                        kind, a, k = l[pos[li]]
                        pos[li] += 1
                        getattr(S_real, kind)(*a, **k)
                        live = True

        ident = A.alloc([128], BF16)
        onesblk = A.alloc([128], BF16)
        cstage = A.alloc([2, 128], F32)
        for i, dst in enumerate((ident, onesblk)):
            S.dma("sp", cstage[:, i, :], cst_d[i, :, :], w=[("cstage", i)])
            S.op("dve", "tensor_copy", dict(out=dst, in_=cstage[:, i, :]), r=[("cstage", i)], w=[("const", i)])
        Bm = A.alloc([N_BT, 128], BF16)
        bst = A.alloc([2, 4, 128], F32)
        bm_todo = list(range(N_BT // 4))

        def bm_step():
            if not bm_todo:
                return
            k = bm_todo.pop(0)
            s_ = k % 2
            S.dma("sp", bst[:, s_, :, :], bias_d[4 * k:4 * k + 4, :, :].rearrange("t p q -> p t q"), w=[("bst", s_)])
            S.op("act", "activation", dict(out=Bm[:, 4 * k:4 * k + 4, :], in_=bst[:, s_, :, :], func=AF.Exp),
                 r=[("bst", s_)], w=[("Bm", k)])
        Bm_keys = []
        persist_off = A.off

        Wb = A.alloc([8, NCOL], BF16)
        wst = A.alloc([4, NCOL], F32)
        gtile = A.alloc([D_MODEL], F32)
        qkg = A.alloc([2], F32)
        xt = A.alloc([8, D_MODEL], F32)
        sqj = A.alloc([D_MODEL], BF16)
        ssb = A.alloc([2, 4], F32)
        rmsb = A.alloc([2, 4], F32)
        u = A.alloc([4, D_MODEL], BF16)
        uT = A.alloc([2, 8, 512], BF16)
        fmst = A.alloc([3, 512], BF16)
        rst = A.alloc([2, 512], F32)
        sqn = A.alloc([2, 512], BF16)
        rmsn = A.alloc([2, 512], F32)
        tmst = A.alloc([2, NTM], BF16)

        S.dma("sp", gtile, gt_d[:, :], w=["gtile"])
        S.dma("sp", qkg, qkg_d[:, :], w=["qkg"])
        for kc in range(8):
            s = kc % 4
            S.dma("sp" if kc % 2 == 0 else "pool", wst[:, s, :], win_d[kc * 128:(kc + 1) * 128, :], w=[("wst", s)])
            h1 = NCOL // 2
            S.op("dve", "tensor_copy", dict(out=Wb[:, kc, 0:h1], in_=wst[:, s, 0:h1]), r=[("wst", s)], w=[("Wb", kc)])
            S.op("pool", "tensor_copy", dict(out=Wb[:, kc, h1:NCOL], in_=wst[:, s, h1:NCOL]), r=[("wst", s)], w=[("Wb", kc, 1)])

        FM_TILES = [(256, 128, "nq0"), (0, 128, "q"), (384, 128, "nq1"), (128, 128, "k"),
                    (512, 128, "nk0"), (768, 32, "r"), (640, 128, "nk1")]
        cn = dict(x=0, u=0, fm=0, r=0, n=0, tm=0, ptr=0, pfm=0, ptm=0)

        def p1_prep_a_items(gi, tiles):
            us = gi % 2
            ntl = len(tiles)
            xbase = cn["x"]
            items = []
            for ti, (src, nt, ltk, xtk) in enumerate(tiles):
                xs = cn["x"] % 8
                cn["x"] += 1
                S.dma("sp", xt[:nt, xs, :], src, w=[("xt", xs)])

                def sq(ti=ti, nt=nt, xs=xs):
                    S.op("act", "activation", dict(out=sqj[:nt, :], in_=xt[:nt, xs, :], func=AF.Square,
                                                   accum_out=ssb[:nt, us, ti:ti + 1]),
                         r=[("xt", xs)], w=["sqj", ("ssb", us, ti)])
                items.append(sq)
            nt0 = tiles[0][1]

            def rs():
                S.op("act", "activation", dict(out=rmsb[:nt0, us, :ntl], in_=ssb[:nt0, us, :ntl], func=AF.Ln,
                                               scale=1.0 / D_MODEL, bias=EPS),
                     r=[("ssb", us, ti) for ti in range(ntl)], w=[("rmsb", us)])
                S.op("act", "activation", dict(out=rmsb[:nt0, us, :ntl], in_=rmsb[:nt0, us, :ntl], func=AF.Exp, scale=-0.5),
                     r=[("rmsb", us)], w=[("rmsb", us)])
            items.append(rs)
            for ti, (src, nt, ltk, xtk) in enumerate(tiles):
                xs = (xbase + ti) % 8

                def uu(ti=ti, nt=nt, xs=xs):
                    S.op("dve", "scalar_tensor_tensor",
                         dict(out=u[:nt, ti, :], in0=xt[:nt, xs, :], scalar=rmsb[:nt, us, ti:ti + 1],
                              in1=gtile[:nt, :], op0=ALU.mult, op1=ALU.mult),
                         r=[("xt", xs), ("rmsb", us), "gtile"], w=[("u", ti)])
                items.append(uu)
            return items

        def p1_prep_b(gi, tiles, ti):
            us = gi % 2
            nt = tiles[ti][1]
            toff = sum(t[1] for t in tiles[:ti])
            pb = cn["ptr"] % 2
            cn["ptr"] += 1
            trv = banksb[pb][:, 0:1024].rearrange("p (a b) -> p a b", a=8)
            for kc in range(8):
                S.op("pe", "transpose", dict(out=trv[:, kc, :nt], in_=u[:nt, ti, kc * 128:(kc + 1) * 128],
                                             identity=ident[:nt, :nt]),
                     r=[("u", ti), ("const", 0)], w=[("ps", pb)], inc=(kc == 7))
            if ti % 2 == 0:
                S.op("act", "activation", dict(out=uT[:, us, :, toff:toff + nt], in_=trv[:, :, :nt], func=AF.Copy),
                     r=[("ps", pb)], w=[("uT", us, ti)])
            else:
                S.op("dve", "tensor_copy", dict(out=uT[:, us, :, toff:toff + nt], in_=trv[:, :, :nt]),
                     r=[("ps", pb)], w=[("uT", us, ti)])

        def p1_main(gi, tiles, hook):
            deferred = []
            T = sum(t[1] for t in tiles)
            us = gi % 2
            ntl = len(tiles)
            uT_keys = [("uT", us, ti) for ti in range(ntl)]
            ltok0 = tiles[0][2]
            xtok0 = tiles[0][3]
            for (c0, rows, kind) in FM_TILES:
                if xtok0 is None and kind in ("q", "k", "nq0", "nq1"):
                    continue
                pb = (2, 3, 7)[cn["pfm"] % 3]
                cn["pfm"] += 1
                for kc in range(8):
                    S.op("pe", "matmul", dict(out=banks[pb][:rows, :T], lhsT=Wb[:, kc, c0:c0 + rows],
                                              rhs=uT[:, us, kc, :T], start=(kc == 0), stop=(kc == 7)),
                         r=uT_keys + [("Wb", kc), ("Wb", kc, 1)], w=[("ps", pb)], inc=(kc == 7))
                while deferred:
                    deferred.pop(0)()
                hook()
                if kind in ("q", "k"):
                    fs = cn["fm"] % 3
                    cn["fm"] += 1
                    S.op("act", "activation", dict(out=fmst[:, fs, :T], in_=banks[pb][:, :T], func=AF.Copy),
                         r=[("ps", pb)], w=[("fmst", fs)])
                    dst = st_qk[:, 0 if kind == "q" else 1, :]
                    S.dma("pool", dst[:, xtok0:xtok0 + T], fmst[:, fs, :T], r=[("fmst", fs)])
                elif kind == "r":
                    rs_ = cn["r"] % 2
                    cn["r"] += 1
                    S.op("dve", "tensor_copy", dict(out=rst[:32, rs_, :T], in_=banks[pb][:32, :T]),
                         r=[("ps", pb)], w=[("rst", rs_)])
                    S.dma("pool", st_rT[:, ltok0:ltok0 + T], rst[:32, rs_, :T], r=[("rst", rs_)])
                else:
                    isq = kind.startswith("nq")
                    half = int(kind[-1])
                    ns = cn["n"] % 2
                    cn["n"] += 1
                    S.op("act", "activation", dict(out=sqn[:, ns, :T], in_=banks[pb][:, :T], func=AF.Square),
                         r=[("ps", pb)], w=[("sqn", ns)])
                    fs = cn["fm"] % 3
                    cn["fm"] += 1

                    def chain(pb=pb, ns=ns, fs=fs, isq=isq, half=half):
                        S.op("pe", "matmul", dict(out=banks[6][:, :T], lhsT=onesblk, rhs=sqn[:, ns, :T],
                                                  start=True, stop=True),
                             r=[("sqn", ns), ("const", 1)], w=[("ps", 6)])
                        S.op("act", "activation", dict(out=rmsn[:, ns, :T], in_=banks[6][:, :T], func=AF.Ln,
                                                       scale=1.0 / 64, bias=EPS),
                             r=[("ps", 6)], w=[("rmsn", ns)])
                        S.op("act", "activation", dict(out=rmsn[:, ns, :T], in_=rmsn[:, ns, :T], func=AF.Exp, scale=-0.5),
                             r=[("rmsn", ns)], w=[("rmsn", ns)])
                        gcol = 0 if isq else 1
                        S.op("dve", "scalar_tensor_tensor",
                             dict(out=fmst[:, fs, :T], in0=banks[pb][:, :T], scalar=qkg[:, gcol:gcol + 1],
                                  in1=rmsn[:, ns, :T], op0=ALU.mult, op1=ALU.mult),
                             r=[("ps", pb), ("rmsn", ns), "qkg"], w=[("fmst", fs)])
                        if isq:
                            S.dma("pool", st_nqT[half * 128:(half + 1) * 128, xtok0:xtok0 + T], fmst[:, fs, :T],
                                  r=[("fmst", fs)])
                        else:
                            S.dma("pool", st_nkT[half * 128:(half + 1) * 128, ltok0:ltok0 + T], fmst[:, fs, :T],
                                  r=[("fmst", fs)])
                    deferred.append(chain)
            toff = 0
            for ti, (src, nt, ltk, xtk) in enumerate(tiles):
                ts_ = cn["tm"] % 2
                cn["tm"] += 1
                for n0 in (0, 384, 768):
                    nsz = 384
                    pb = 4 + cn["ptm"] % 2
                    cn["ptm"] += 1
                    for kc in range(8):
                        S.op("pe", "matmul", dict(out=banks[pb][:nt, :nsz], lhsT=uT[:, us, kc, toff:toff + nt],
                                                  rhs=Wb[:, kc, NFM + n0:NFM + n0 + nsz], start=(kc == 0), stop=(kc == 7)),
                             r=[("uT", us, ti), ("Wb", kc), ("Wb", kc, 1)], w=[("ps", pb)], inc=(kc == 7))
                    while deferred:
                        deferred.pop(0)()
                    plain_hi = min(max(640 - n0, 0), nsz)
                    if plain_hi > 0:
                        S.op("dve", "tensor_copy", dict(out=tmst[:nt, ts_, n0:n0 + plain_hi], in_=banks[pb][:nt, 0:plain_hi]),
                             r=[("ps", pb)], w=[("tmst", ts_, n0, 0)])
                    if plain_hi < nsz:
                        S.op("act", "activation", dict(out=tmst[:nt, ts_, n0 + plain_hi:n0 + nsz], in_=banks[pb][:nt, plain_hi:nsz],
                                                       func=AF.Silu),
                             r=[("ps", pb)], w=[("tmst", ts_, n0, 1)])
                hook()
                S.dma("pool", st_tm[ltk:ltk + nt, :], tmst[:nt, ts_, :],
                      r=[("tmst", ts_, 0, 0), ("tmst", ts_, 384, 0), ("tmst", ts_, 384, 1), ("tmst", ts_, 768, 1)])
                toff += nt

        groups = [[(meta_d[:, :], N_META, 0, None)]]
        for g in range(seq // 512):
            groups.append([(x_d[g * 512 + i * 128: g * 512 + (i + 1) * 128, :], 128,
                            N_META + g * 512 + i * 128, g * 512 + i * 128) for i in range(4)])
        for it in p1_prep_a_items(0, groups[0]):
            it()
        for ti in range(len(groups[0])):
            p1_prep_b(0, groups[0], ti)
        for gi in range(len(groups)):
            pending = []
            if gi + 1 < len(groups):
                pending = p1_prep_a_items(gi + 1, groups[gi + 1])
                pending += [(lambda g2=gi + 1, t2=ti: p1_prep_b(g2, groups[g2], t2)) for ti in range(len(groups[gi + 1]))]

            def hook():
                if pending:
                    pending.pop(0)()
            bm_step()
            bm_step()
            p1_main(gi, groups[gi], hook)
            while pending:
                pending.pop(0)()
        while bm_todo:
            bm_step()
        S.barrier()

        A.off = persist_off
        NL = 7
        NP = 3
        tri = A.alloc([6, 128], F32)
        wdec = A.alloc([256], F32)
        gout = A.alloc([256], F32)
        ofall = A.alloc([ntile, 256], F32)
        S32 = A.alloc([2, 256], F32)
        Sb = A.alloc([2, 256], BF16)
        qkT = A.alloc([2, NL, 256], BF16)
        raug = A.alloc([2, NL, 128], F32)
        tmkv = A.alloc([2, NL, 896], BF16)
        r_hi = A.alloc([2, 2, 128], BF16)
        r_lo = A.alloc([2, 2, 128], BF16)
        e1 = A.alloc([2, 2, 128], F32)
        lsp = A.alloc([2, 2, 128], F32)
        l_hi = A.alloc([2, NP, 128], BF16)
        l_lo = A.alloc([2, NP, 128], BF16)
        EG = A.alloc([2, NP, 128], F32)
        EnG = A.alloc([2, 2, 128], F32)
        Est = A.alloc([2, 2, 128], F32)
        kin = A.alloc([2, 2, 128], BF16)
        kst = A.alloc([2, 2, 128], BF16)
        qin = A.alloc([2, NP, 128], BF16)
        ATm = A.alloc([2, NP, 128], BF16)
        Usb = A.alloc([2, NP, 256], F32)
        osum = A.alloc([2, 2, 256], F32)
        gsg = A.alloc([2, 2, 256], F32)
        ogb = A.alloc([2, 2, 256], BF16)
        oTs = A.alloc([2, 2, 256], BF16)
        sqj2 = A.alloc([256], BF16)
        ssq = A.alloc([2, 2, 1], F32)
        lnv = A.alloc([2, 2, 1], F32)
        rstd = A.alloc([2, 2, 1], F32)
        trib = A.alloc([6, 128], BF16)
        wd_hi = A.alloc([256], BF16)
        wd_lo = A.alloc([256], BF16)

        S.dma("sp", tri, cst_d[2:8, :, :].rearrange("t p q -> p t q"), w=["tri"])
        S.dma("sp", wdec[:17, :], wdec_d[:, :], w=["wdec"])
        S.dma("sp", gout, gout_d[:, :], w=["gout"])
        S.op("dve", "tensor_copy", dict(out=trib, in_=tri), r=["tri"], w=["trib"])
        S.op("dve", "tensor_copy", dict(out=wd_hi[:17, :], in_=wdec[:17, :]), r=["wdec"], w=["wd_hi"])
        S.op("dve", "tensor_tensor", dict(out=wd_lo[:17, :], in0=wdec[:17, :], in1=wd_hi[:17, :], op=ALU.subtract),
             r=["wdec", "wd_hi"], w=["wd_lo"])
        for d in range(2):
            for sl in range(NL):
                S.op("pool", "memset", dict(ap=raug[:17, d, sl, :], constant=1.0), w=[("raug", d, sl)])
        S.op("pool", "memset", dict(ap=S32[:, 1, :], constant=0.0), w=[("S32", 1)])
        S.op("pool", "memset", dict(ap=Sb[:, 1, :], constant=0.0), w=[("Sb", 1)])

        LG = (0, 4)
        GG = (1, 5)
        AU = (2, 6)
        OB = (3, 7)
        QSCALE = 128 ** -0.5

        def logits_mm(d, ts, ncol, rsrc, rkeys, outreg, outkey):
            S.op("dve", "tensor_copy", dict(out=r_hi[:17, d, ts, :ncol], in_=rsrc), r=rkeys, w=[("r_hi", d, ts)])
            S.op("dve", "tensor_tensor", dict(out=r_lo[:17, d, ts, :ncol], in0=rsrc, in1=r_hi[:17, d, ts, :ncol],
                                              op=ALU.subtract), r=rkeys + [("r_hi", d, ts)], w=[("r_lo", d, ts)])
            wsl = slice(128 * d, 128 * d + 128)
            S.op("pe", "matmul", dict(out=outreg, lhsT=r_hi[:17, d, ts, :ncol], rhs=wd_hi[:17, wsl], start=True, stop=False),
                 r=[("r_hi", d, ts), "wd_hi"], w=[outkey])
            S.op("pe", "matmul", dict(out=outreg, lhsT=r_lo[:17, d, ts, :ncol], rhs=wd_hi[:17, wsl], start=False, stop=False),
                 r=[("r_lo", d, ts), "wd_hi"], w=[outkey])
            S.op("pe", "matmul", dict(out=outreg, lhsT=r_hi[:17, d, ts, :ncol], rhs=wd_lo[:17, wsl], start=False, stop=True),
                 r=[("r_hi", d, ts), "wd_lo"], w=[outkey])

        def softplus_split(d, ts, ps_, npart, lgreg, lgkey):
            S.op("act", "activation", dict(out=e1[:npart, d, ts, :], in_=lgreg, func=AF.Exp, scale=-1.0),
                 r=[lgkey], w=[("e1", d, ts)])
            S.op("act", "activation", dict(out=lsp[:npart, d, ts, :], in_=e1[:npart, d, ts, :], func=AF.Ln, bias=1.0),
                 r=[("e1", d, ts)], w=[("lsp", d, ts)])
            S.op("dve", "tensor_copy", dict(out=l_hi[:npart, d, ps_, :], in_=lsp[:npart, d, ts, :]),
                 r=[("lsp", d, ts)], w=[("l_hi", d, ps_)])
            S.op("dve", "tensor_tensor", dict(out=l_lo[:npart, d, ps_, :], in0=lsp[:npart, d, ts, :],
                                              in1=l_hi[:npart, d, ps_, :], op=ALU.subtract),
                 r=[("lsp", d, ts), ("l_hi", d, ps_)], w=[("l_lo", d, ps_)])

        S.dma("sp", raug[:16, 0, 0, 0:16], st_rT[0:16, 0:16], w=[("raug", 0, 0)])
        S.dma("sp", tmkv[:16, 0, 0, 0:384], st_tm[0:16, 0:384], w=[("tmkv", 0, 0)])
        logits_mm(0, 0, 16, raug[:17, 0, 0, 0:16], [("raug", 0, 0)], banks[LG[0]][:16, 0:128], ("ps", LG[0]))
        softplus_split(0, 0, 0, 16, banks[LG[0]][:16, 0:128], ("ps", LG[0]))
        S.op("pe", "matmul", dict(out=banks[GG[0]][:16, 0:128], lhsT=trib[:16, 1, 0:16], rhs=l_hi[:16, 0, 0, :],
                                  start=True, stop=False), r=[("l_hi", 0, 0), "trib"], w=[("ps", GG[0])], inc=False)
        S.op("pe", "matmul", dict(out=banks[GG[0]][:16, 0:128], lhsT=trib[:16, 1, 0:16], rhs=l_lo[:16, 0, 0, :],
                                  start=False, stop=True), r=[("l_lo", 0, 0), "trib"], w=[("ps", GG[0])])
        S.op("act", "activation", dict(out=Est[:16, 0, 0, :], in_=banks[GG[0]][:16, 0:128], func=AF.Exp),
             r=[("ps", GG[0])], w=[("Est", 0, 0)])
        S.op("dve", "tensor_tensor", dict(out=kst[:16, 0, 0, :], in0=tmkv[:16, 0, 0, 0:128], in1=Est[:16, 0, 0, :],
                                          op=ALU.mult), r=[("tmkv", 0, 0), ("Est", 0, 0)], w=[("kst", 0, 0)])
        S.op("pe", "matmul", dict(out=banks[AU[0]][:, 128:384], lhsT=kst[:16, 0, 0, :], rhs=tmkv[:16, 0, 0, 128:384],
                                  start=True, stop=True), r=[("kst", 0, 0), ("tmkv", 0, 0)], w=[("ps", AU[0])])
        S.op("dve", "tensor_copy", dict(out=S32[:, 0, :], in_=banks[AU[0]][:, 128:384]), r=[("ps", AU[0])], w=[("S32", 0)])
        S.op("act", "activation", dict(out=Sb[:, 0, :], in_=banks[AU[0]][:, 128:384], func=AF.Copy),
             r=[("ps", AU[0])], w=[("Sb", 0)])

        gdone = {}
        do_cc = debug not in (3, 4)

        def chunk_of(d, i):
            return i if d == 0 else ntile - 1 - i

        def is_second(i):
            return i >= ntile // 2

        def stage_L(d, i):
            c = chunk_of(d, i)
            sl = i % NL
            lt = N_META + c * 128
            xt0 = c * 128
            S.dma("sp", raug[:16, d, sl, :], st_rT[16 * d:16 * d + 16, lt:lt + 128], w=[("raug", d, sl)])
            S.dma("sp", qkT[:, d, sl, :].rearrange("p (a b) -> p a b", a=2), st_qk[:, :, xt0:xt0 + 128],
                  w=[("qTt", d, sl), ("kTt", d, sl)])
            S.dma("sp", tmkv[:, d, sl, :], st_tm[lt:lt + 128, 0:896], w=[("tmkv", d, sl)])

        def stage_A1(d, i):
            sl, ts = i % NL, i % 2
            logits_mm(d, ts, 128, raug[:17, d, sl, :], [("raug", d, sl)], banks[LG[d]][:, 128 * ts:128 * ts + 128], ("ps", LG[d]))

        def stage_A2(d, i):
            ts, ps_ = i % 2, i % NP
            softplus_split(d, ts, ps_, 128, banks[LG[d]][:, 128 * ts:128 * ts + 128], ("ps", LG[d]))

        def stage_B1(d, i):
            sl, ts, ps_ = i % NL, i % 2, i % NP
            gg, au = GG[d], AU[d]
            S.op("pe", "matmul", dict(out=banks[gg][:, 0:128], lhsT=trib[:, 3 * d + 1, :], rhs=l_hi[:, d, ps_, :],
                                      start=True, stop=False), r=[("l_hi", d, ps_), "trib"], w=[("ps", gg)], inc=False)
            S.op("pe", "matmul", dict(out=banks[gg][:, 0:128], lhsT=trib[:, 3 * d + 1, :], rhs=l_lo[:, d, ps_, :],
                                      start=False, stop=True), r=[("l_lo", d, ps_), "trib"], w=[("ps", gg)], inc=False)
            S.op("pe", "matmul", dict(out=banks[gg][:, 128:256], lhsT=l_hi[:, d, ps_, :], rhs=trib[:, 3 * d, :],
                                      start=True, stop=False), r=[("l_hi", d, ps_), "trib"], w=[("ps", gg)], inc=False)
            S.op("pe", "matmul", dict(out=banks[gg][:, 128:256], lhsT=l_lo[:, d, ps_, :], rhs=trib[:, 3 * d, :],
                                      start=False, stop=True), r=[("l_lo", d, ps_), "trib"], w=[("ps", gg)])
            S.op("act", "activation", dict(out=EG[:, d, ps_, :], in_=banks[gg][:, 128:256], func=AF.Exp),
                 r=[("ps", gg)], w=[("EG", d, ps_)])
            S.op("act", "activation", dict(out=EnG[:, d, ts, :], in_=banks[gg][:, 128:256], func=AF.Exp, scale=-1.0),
                 r=[("ps", gg)], w=[("EnG", d, ts)])
            S.op("act", "activation", dict(out=Est[:, d, ts, :], in_=banks[gg][:, 0:128], func=AF.Exp),
                 r=[("ps", gg)], w=[("Est", d, ts)])

        def stage_B2(d, i):
            sl, ts, ps_ = i % NL, i % 2, i % NP
            gg, au = GG[d], AU[d]
            S.op("dve", "scalar_tensor_tensor", dict(out=qin[:, d, ps_, :], in0=qkT[:, d, sl, 0:128], scalar=QSCALE,
                                                     in1=EG[:, d, ps_, :], op0=ALU.mult, op1=ALU.mult),
                 r=[("qTt", d, sl), ("EG", d, ps_)], w=[("qin", d, ps_)])
            S.op("pool", "tensor_tensor", dict(out=kin[:, d, ts, :], in0=qkT[:, d, sl, 128:256], in1=EnG[:, d, ts, :], op=ALU.mult),
                 r=[("kTt", d, sl), ("EnG", d, ts)], w=[("kin", d, ts)])
            S.op("pool", "tensor_tensor", dict(out=kst[:, d, ts, :], in0=tmkv[:, d, sl, 0:128], in1=Est[:, d, ts, :], op=ALU.mult),
                 r=[("tmkv", d, sl), ("Est", d, ts)], w=[("kst", d, ts)])
            S.op("pe", "matmul", dict(out=banks[au][:, 0:128], lhsT=kin[:, d, ts, :], rhs=qin[:, d, ps_, :],
                                      start=True, stop=True),
                 r=[("kin", d, ts), ("qin", d, ps_)], w=[("ps", au)], inc=False)
            S.op("pe", "matmul", dict(out=banks[au][:, 128:384], lhsT=kst[:, d, ts, :], rhs=tmkv[:, d, sl, 128:384],
                                      start=True, stop=True), r=[("kst", d, ts), ("tmkv", d, sl)], w=[("ps", au)])
            S.op("dve", "tensor_tensor", dict(out=ATm[:, d, ps_, :], in0=banks[au][:, 0:128], in1=trib[:, 3 * d + 2, :],
                                              op=ALU.mult), r=[("ps", au), "trib"], w=[("ATm", d, ps_)])
            S.op("act", "activation", dict(out=Usb[:, d, ps_, :], in_=banks[au][:, 128:384], func=AF.Copy),
                 r=[("ps", au)], w=[("Usb", d, ps_)])

        def stage_C(d, i):
            c = chunk_of(d, i)
            sl, ts, ps_ = i % NL, i % 2, i % NP
            ob = OB[d]
            last = 127 if d == 0 else 0
            S.op("pe", "matmul", dict(out=banks[ob][:, 0:256], lhsT=ATm[:, d, ps_, :], rhs=tmkv[:, d, sl, 128:384],
                                      start=True, stop=False),
                 r=[("ATm", d, ps_), ("tmkv", d, sl)], w=[("ps", ob)], inc=False)
            S.op("pe", "matmul", dict(out=banks[ob][:, 0:256], lhsT=qin[:, d, ps_, :], rhs=Sb[:, d, :], start=False, stop=True),
                 r=[("qin", d, ps_), ("Sb", d)], w=[("ps", ob)])
            S.op("dve", "scalar_tensor_tensor", dict(out=S32[:, d, :], in0=S32[:, d, :], scalar=EG[:, d, ps_, last:last + 1],
                                                     in1=Usb[:, d, ps_, :], op0=ALU.mult, op1=ALU.add),
                 r=[("S32", d), ("EG", d, ps_), ("Usb", d, ps_)], w=[("S32", d)])
            S.op("pool", "tensor_copy", dict(out=Sb[:, d, :], in_=S32[:, d, :]), r=[("S32", d)], w=[("Sb", d)])
            if not is_second(i):
                S.op("act", "activation", dict(out=ofall[:, c, :], in_=banks[ob][:, 0:256], func=AF.Copy),
                     r=[("ps", ob)], w=[("ofall", c)])
            else:
                S.op("dve", "tensor_tensor", dict(out=osum[:, d, ts, :], in0=banks[ob][:, 0:256], in1=ofall[:, c, :], op=ALU.add),
                     r=[("ps", ob), ("ofall", c)], w=[("osum", d, ts)])

        def stage_D(d, i):
            if not is_second(i):
                return
            c = chunk_of(d, i)
            sl, ts = i % NL, i % 2
            ob = OB[d]
            xt0 = c * 128
            S.op("act", "activation", dict(out=sqj2, in_=osum[:, d, ts, :], func=AF.Square, accum_out=ssq[:, d, ts, :]),
                 r=[("osum", d, ts)], w=["sqj2", ("ssq", d, ts)])
            S.op("act", "activation", dict(out=lnv[:, d, ts, :], in_=ssq[:, d, ts, :], func=AF.Ln, scale=1.0 / 256, bias=EPS),
                 r=[("ssq", d, ts)], w=[("lnv", d, ts)])
            S.op("act", "activation", dict(out=rstd[:, d, ts, :], in_=lnv[:, d, ts, :], func=AF.Exp, scale=-0.5),
                 r=[("lnv", d, ts)], w=[("rstd", d, ts)])
            S.op("pool", "tensor_tensor", dict(out=gsg[:, d, ts, :], in0=gout, in1=tmkv[:, d, sl, 640:896], op=ALU.mult),
                 r=["gout", ("tmkv", d, sl)], w=[("gsg", d, ts)])
            S.op("dve", "scalar_tensor_tensor", dict(out=ogb[:, d, ts, :], in0=osum[:, d, ts, :], scalar=rstd[:, d, ts, :],
                                                     in1=gsg[:, d, ts, :], op0=ALU.mult, op1=ALU.mult),
                 r=[("osum", d, ts), ("rstd", d, ts), ("gsg", d, ts)], w=[("ogb", d, ts)])

        def stage_D2(d, i):
            if not is_second(i):
                return
            c = chunk_of(d, i)
            ts = i % 2
            xt0 = c * 128
            tb = LG[d]
            trv = banksb[tb][:, 512:768].rearrange("p (a b) -> p a b", a=2)
            for hf in range(2):
                S.op("pe", "transpose", dict(out=trv[:, hf, :], in_=ogb[:, d, ts, hf * 128:(hf + 1) * 128], identity=ident),
                     r=[("ogb", d, ts), ("const", 0)], w=[("ps", tb)], inc=(hf == 1))
            oTv = oTs[:, d, ts, :].rearrange("p (a b) -> p a b", a=2)
            S.op("act", "activation", dict(out=oTv, in_=trv, func=AF.Copy), r=[("ps", tb)], w=[("oTs", d, ts)])
            S.dma("act", ex_in_ap(0, 256, xt0, 128).rearrange("(a p) t -> p a t", p=128), oTv,
                  r=[("oTs", d, ts)], w=[("exin", xt0 // XC, "g", c)])
            kch = xt0 // XC
            gdone[kch] = gdone.get(kch, 0) + 1
            if do_cc and gdone[kch] == XC // 128:
                keys = [("exin", kch, "g", cc_) for cc_ in range(kch * (XC // 128), (kch + 1) * (XC // 128))]
                if debug:
                    S.dma("pool", dbg_ex[0:256, kch * XC:(kch + 1) * XC], exg_in_l[kch][:, :], r=keys)
                ccg[kch] = S.collective(dict(kind="AllGather", op=ALU.bypass, replica_groups=RG,
                                             ins=[exg_in_l[kch].ap().opt()], outs=[exg_out_l[kch].ap().opt()]), r=keys)

        for t in range(-6, ntile):
            th = []
            for d in range(2):
                if 0 <= t + 6 < ntile:
                    th.append(lambda d=d, i=t + 6: stage_L(d, i))
            for d in range(2):
                if 0 <= t + 4 < ntile:
                    th.append(lambda d=d, i=t + 4: stage_A1(d, i))
            for d in range(2):
                if 0 <= t + 3 < ntile:
                    th.append(lambda d=d, i=t + 3: stage_A2(d, i))
            for d in range(2):
                if 0 <= t + 2 < ntile:
                    th.append(lambda d=d, i=t + 2: stage_B1(d, i))
            for d in range(2):
                if 0 <= t + 1 < ntile:
                    th.append(lambda d=d, i=t + 1: stage_B2(d, i))
            for d in range(2):
                if 0 <= t < ntile:
                    th.append(lambda d=d, i=t: (stage_C(d, i), stage_D(d, i)))
            interleave(th)
            for d in range(2):
                if 0 <= t - 1 < ntile:
                    stage_D2(d, t - 1)
        for d in range(2):
            stage_D2(d, ntile - 1)
        S.barrier()
        gla_peak = A.off
        na_peak = 0
        do_na = debug != 3
        do_cc = debug not in (3, 4)

        if do_na:
            A.off = persist_off
            RING = 16
            NQ = 7
            LA = 5
            mbias = A.alloc([4], F32)
            nkm2 = A.alloc([2, 128], BF16)
            vma = A.alloc([4, 65], BF16)
            nkr2 = A.alloc([RING, 2, 128], BF16)
            var_ = A.alloc([RING, 4, 65], BF16)
            nqb = A.alloc([NQ, 2, 256], BF16)
            sng = A.alloc([NQ, 256], BF16)
            Et = A.alloc([3, 6, 256], BF16)
            Etm = A.alloc([3, 5, 256], BF16)
            emb = A.alloc([4], F32)
            rec = A.alloc([2, 4, 1], F32)
            ona = A.alloc([2, 256], BF16)
            NOS = 8
            oT2 = A.alloc([NOS, 2, 128], BF16)
            NOT = 4
            OT = A.alloc([NOT, 16, 512], BF16)
            ot_loaded = set()

            def load_OT(g):
                if g in ot_loaded or g >= seq // 512:
                    return
                ot_loaded.add(g)
                kch = (g * 512) // XC
                o = (g * 512) % XC
                S._wait("sp", ccg[kch])
                S.dma("sp", OT[:, g % NOT, 0:8, :], exg_out_l[kch][:, o:o + 512].rearrange("(a p) t -> p a t", p=128),
                      w=[("OT", g % NOT)])
                S._wait("pool", ccn[kch])
                S.dma("pool", OT[:, g % NOT, 8:16, :], exn_out_l[kch][:, o:o + 512].rearrange("(a p) t -> p a t", p=128),
                      w=[("OTb", g % NOT)])

            S.dma("sp", mbias[:16, :], mb_d[:, :], w=["mbias"])
            S.op("pool", "memset", dict(ap=nkm2, constant=0.0), w=["nkm2"])
            S.dma("sp", nkm2[:, :, 0:16], st_nkT[:, 0:16].rearrange("(pr q) t -> q pr t", q=128), r=["nkm2"], w=["nkm2p"])
            S.op("pool", "memset", dict(ap=vma[:16, :, :], constant=1.0), w=["vma"])
            S.dma("sp", vma[:16, :, 0:64], st_tm[0:16, 384:640].rearrange("t (h e) -> t h e", h=4), w=["vma"])
            S.op("act", "activation", dict(out=emb[:16, :], in_=mbias[:16, :], func=AF.Exp), r=["mbias"], w=["emb"])
            for h in range(4):
                S.op("dve", "tensor_scalar", dict(out=vma[:16, h, :], in0=vma[:16, h, :], scalar1=emb[:16, h:h + 1],
                                                  scalar2=None, op0=ALU.mult), r=["vma", "emb"], w=["vma"])
            S.op("pool", "memset", dict(ap=var_[:, :, :, 64:65].rearrange("p r h e -> p (r h) e"), constant=1.0),
                 w=[("var", sl) for sl in range(RING)])
            S.op("dve", "memset", dict(ap=nqb, constant=0.0), w=[("nqb", sl) for sl in range(NQ)])

            loaded = set()

            def load_key_tile(t):
                if t in loaded:
                    return
                loaded.add(t)
                sl = t % RING
                lt = N_META + t * 128
                S.dma("pool", nkr2[:, sl, :, :], st_nkT[:, lt:lt + 128].rearrange("(pr q) t -> q pr t", q=128), w=[("nkr", sl)])
                S.dma("pool", var_[:, sl, :, 0:64], st_tm[lt:lt + 128, 384:640].rearrange("t (h e) -> t h e", h=4),
                      w=[("var", sl)])

            def rs_of(r):
                return min(max(r - 4, 0), nrows - 8)

            edge_ms = (0, 1, ntile - 2, ntile - 1)

            def tiles_of(m):
                t0 = rs_of(2 * m) // 2
                t1 = (rs_of(2 * m + 1) + 7) // 2
                tl = list(range(t0, t1 + 1))
                assert len(tl) <= 5
                return tl

            def na_L(m):
                ms = m % NQ
                for t in tiles_of(m):
                    load_key_tile(t)
                for pr in range(2):
                    for e in range(2):
                        h = 2 * pr + e
                        S.dma("sp", nqb[64 * e:64 * e + 64, ms, pr, 128 * e:128 * e + 128],
                              st_nqT[64 * h:64 * h + 64, m * 128:(m + 1) * 128], r=[("nqb", ms)], w=[("nqbp", ms, h)])
                S.dma("sp", sng[:, ms, :], st_tm[N_META + m * 128:N_META + (m + 1) * 128, 896:1152], w=[("sng", ms)])

            def nq_keys(ms):
                return [("nqb", ms)] + [("nqbp", ms, h) for h in range(4)]

            def bidx_of(m, h):
                if m in edge_ms:
                    return 20 + (edge_ms.index(m) * 4 + h) * 4
                return h * 5

            def na_X1(kp):
                m, pr = kp // 2, kp % 2
                ms = m % NQ
                hs = kp % 2
                es = kp % 3
                tiles = tiles_of(m)
                ntk = len(tiles)
                bA, bB, bC = 3 * hs, 3 * hs + 1, 3 * hs + 2
                for ti, t in enumerate(tiles):
                    bk = (bA, bA, bB, bB, bC)[ti]
                    reg = banks[bk][:, (ti % 2) * 256:(ti % 2) * 256 + 256] if ti < 4 else banks[bC][:, 0:256]
                    last_in_bank = (ti in (1, 3)) or (ti == ntk - 1)
                    S.op("pe", "matmul", dict(out=reg, lhsT=nkr2[:, t % RING, pr, :], rhs=nqb[:, ms, pr, :],
                                              start=True, stop=True),
                         r=[("nkr", t % RING)] + nq_keys(ms), w=[("ps", bk)], inc=last_in_bank)
                S.op("pe", "matmul", dict(out=banks[bC][:, 256:512], lhsT=nkm2[:, pr, :], rhs=nqb[:, ms, pr, :],
                                          start=True, stop=True),
                     r=["nkm2", "nkm2p"] + nq_keys(ms), w=[("ps", bC)])

            def na_X2(kp):
                m, pr = kp // 2, kp % 2
                ms = m % NQ
                hs = kp % 2
                es = kp % 3
                tiles = tiles_of(m)
                ntk = len(tiles)
                bA, bB, bC = 3 * hs, 3 * hs + 1, 3 * hs + 2
                S.op("act", "activation", dict(out=Et[:, es, 0:2, :], in_=banks[bA][:, 0:512].rearrange("p (a b) -> p a b", a=2),
                                               func=AF.Exp, scale=0.125), r=[("ps", bA)], w=[("Et", es)])
                n2 = min(ntk, 4) - 2
                S.op("act", "activation", dict(out=Et[:, es, 2:2 + n2, :],
                                               in_=banks[bB][:, 0:256 * n2].rearrange("p (a b) -> p a b", a=n2),
                                               func=AF.Exp, scale=0.125), r=[("ps", bB)], w=[("Et", es)])
                if ntk == 5:
                    S.op("act", "activation", dict(out=Et[:, es, 4:6, :], in_=banks[bC][:, 0:512].rearrange("p (a b) -> p a b", a=2),
                                                   func=AF.Exp, scale=0.125), r=[("ps", bC)], w=[("Et", es)])
                else:
                    S.op("act", "activation", dict(out=Et[:, es, 5, :], in_=banks[bC][:, 256:512],
                                                   func=AF.Exp, scale=0.125), r=[("ps", bC)], w=[("Et", es)])

            def na_X3(kp):
                m, pr = kp // 2, kp % 2
                ms = m % NQ
                hs = kp % 2
                es = kp % 3
                tiles = tiles_of(m)
                ntk = len(tiles)
                bA, bB, bC = 3 * hs, 3 * hs + 1, 3 * hs + 2
                for e in range(2):
                    h = 2 * pr + e
                    b0 = bidx_of(m, h)
                    meng = "dve"
                    S.op(meng, "tensor_tensor", dict(out=Etm[:, es, 0:ntk, 128 * e:128 * e + 128],
                                                     in0=Et[:, es, 0:ntk, 128 * e:128 * e + 128],
                                                     in1=Bm[:, b0:b0 + ntk, :], op=ALU.mult),
                         r=[("Et", es)] + Bm_keys, w=[("Etm", es, e)])

            def na_Y(kp):
                m, pr = kp // 2, kp % 2
                es = kp % 3
                oab = 6 + m % 2
                tiles = tiles_of(m)
                for e in range(2):
                    h = 2 * pr + e
                    oreg = banks[oab][:, h * 65:(h + 1) * 65]
                    for ti, t in enumerate(tiles):
                        S.op("pe", "matmul", dict(out=oreg, lhsT=Etm[:, es, ti, 128 * e:128 * e + 128], rhs=var_[:, t % RING, h, :],
                                                  start=(ti == 0), stop=False),
                             r=[("Etm", es, e), ("var", t % RING)], w=[("ps", oab)], inc=False)
                    S.op("pe", "matmul", dict(out=oreg, lhsT=Et[:16, es, 5, 128 * e:128 * e + 128], rhs=vma[:16, h, :],
                                              start=False, stop=True),
                         r=[("Et", es), "vma"], w=[("ps", oab)])
                if pr == 1:
                    na_Z(m)

            def na_Z(m):
                ms = m % NQ
                m2 = m % 2
                oab = 6 + m2
                oav = banks[oab][:, 0:260].rearrange("p (h e) -> p h e", h=4)
                S.op("dve", "reciprocal", dict(out=rec[:, m2, :, :], in_=oav[:, :, 64:65]), r=[("ps", oab)], w=[("rec", m2)])
                onv = ona[:, m2, :].rearrange("p (h e) -> p h e", h=4)
                S.op("dve", "tensor_tensor", dict(out=onv, in0=oav[:, :, 0:64], in1=rec[:, m2, :, :].to_broadcast([128, 4, 64]),
                                                  op=ALU.mult),
                     r=[("ps", oab), ("rec", m2)], w=[("ona", m2, 9)])
                S.op("dve", "tensor_tensor", dict(out=ona[:, m2, :], in0=ona[:, m2, :], in1=sng[:, ms, :], op=ALU.mult),
                     r=[("ona", m2, 9), ("sng", ms)], w=[("ona", m2, hh) for hh in range(4)])

            def na_Z2(m):
                m2 = m % 2
                oab = 6 + m2
                trv = banksb[oab][:, 768:1024].rearrange("p (a b) -> p a b", a=2)
                for hf in range(2):
                    S.op("pe", "transpose", dict(out=trv[:, hf, :], in_=ona[:, m2, hf * 128:(hf + 1) * 128], identity=ident),
                         r=[("ona", m2, 2 * hf), ("ona", m2, 2 * hf + 1), ("const", 0)], w=[("ps", oab)], inc=(hf == 1))
                os_ = m % NOS
                S.op("act", "activation", dict(out=oT2[:, os_, :, :], in_=trv, func=AF.Copy), r=[("ps", oab)], w=[("oT2", os_)])
                S.dma("act", ex_in_ap(256, 512, m * 128, 128).rearrange("(a p) t -> p a t", p=128), oT2[:, os_, :, :],
                      r=[("oT2", os_)], w=[("exin", (m * 128) // XC, "n", m)])

            def na_cc(m):
                if do_cc and (m + 1) % (XC // 128) == 0:
                    k = (m * 128) // XC
                    keys = [("exin", k, "n", mm) for mm in range(k * (XC // 128), (k + 1) * (XC // 128))]
                    if debug:
                        S.dma("pool", dbg_ex[256:512, k * XC:(k + 1) * XC], exn_in_l[k][:, :], r=keys)
                    ccn[k] = S.collective(dict(kind="AllGather", op=ALU.bypass, replica_groups=RG,
                                               ins=[exn_in_l[k].ap().opt()], outs=[exn_out_l[k].ap().opt()]), r=keys)
                    if False and k >= 2:
                        for g in (2 * (k - 2), 2 * (k - 2) + 1):
                            if g < NOT:
                                load_OT(g)

            nk_tot = ntile * 2
            for m0 in range(min(LA, ntile)):
                na_L(m0)
            zq = []
            for kp in range(-3, nk_tot):
                th = []
                if kp >= 0 and kp % 2 == 0 and kp // 2 + LA < ntile:
                    th.append(lambda m=kp // 2 + LA: na_L(m))
                if 0 <= kp + 3 < nk_tot:
                    th.append(lambda kk=kp + 3: na_X1(kk))
                if 0 <= kp + 2 < nk_tot:
                    th.append(lambda kk=kp + 2: na_X2(kk))
                if 0 <= kp + 1 < nk_tot:
                    th.append(lambda kk=kp + 1: na_X3(kk))
                if 0 <= kp < nk_tot:
                    th.append(lambda kk=kp: na_Y(kk))
                interleave(th)
                if zq and zq[0][0] <= kp:
                    mz = zq.pop(0)[1]
                    na_Z2(mz)
                    na_cc(mz)
                if kp >= 0 and kp % 2 == 1:
                    zq.append((kp + 1, kp // 2))
            while zq:
                mz = zq.pop(0)[1]
                na_Z2(mz)
                na_cc(mz)
            if not do_cc:
                S.barrier()
            na_peak = A.off

        if debug in (3, 4):
            nr = 256 if debug == 3 else 512
            for k in range(nxc):
                S.dma("pool", dbg_ex[0:256, k * XC:(k + 1) * XC], exg_in_l[k][:, :])
                if nr == 512:
                    S.dma("pool", dbg_ex[256:512, k * XC:(k + 1) * XC], exn_in_l[k][:, :])
            S.barrier()
        if do_cc:

            Wo = A.alloc([16, 256], BF16)
            wost = A.alloc([2, 4, 256], F32)
            xr = A.alloc([4, 512], F32)
            osb = A.alloc([4, 512], F32)
            for k in range(4):
                s = k % 2
                S.dma("sp", wost[:, s, :, :], wout_d[512 * k:512 * (k + 1), :].rearrange("(a p) n -> p a n", p=128),
                      w=[("wost", s)])
                S.op("dve", "tensor_copy", dict(out=Wo[:, 4 * k:4 * k + 4, :], in_=wost[:, s, :, :]),
                     r=[("wost", s)], w=[("Wo", k)])
            Wo_keys = [("Wo", k) for k in range(4)]
            tcount = 0
            for g in range(seq // 512):
                gs = g % NOT
                load_OT(g)
                for ch in range(2):
                    xs = tcount % 4
                    pb = tcount % 2
                    tcount += 1
                    S.dma("sp", xr[:, xs, :], xres_d[ch * 128:(ch + 1) * 128, g * 512:(g + 1) * 512], w=[("xr", xs)])
                    for kc in range(16):
                        S.op("pe", "matmul", dict(out=banks[pb][:, 0:512], lhsT=Wo[:, kc, ch * 128:(ch + 1) * 128],
                                                  rhs=OT[:, gs, kc, :], start=(kc == 0), stop=(kc == 15)),
                             r=[("OT", gs), ("OTb", gs)] + Wo_keys, w=[("ps", pb)], inc=(kc == 15))
                    S.op("dve", "tensor_tensor", dict(out=osb[:, xs, :], in0=banks[pb][:, 0:512], in1=xr[:, xs, :], op=ALU.add),
                         r=[("ps", pb), ("xr", xs)], w=[("osb", xs)])
                    S.dma("act", out_d[ch * 128:(ch + 1) * 128, g * 512:(g + 1) * 512], osb[:, xs, :], r=[("osb", xs)])
            S.barrier(with_cc=True)

        with nc.Block() as block:
            @block.tensor
            def _(e):
                S.emit("pe", e)

            @block.scalar
            def _(e):
                S.emit("act", e)

            @block.vector
            def _(e):
                S.emit("dve", e)

            @block.gpsimd
            def _(e):
                S.emit("pool", e)

            @block.sync
            def _(e):
                S.emit("sp", e)
        info = dict(nops={e: len(S.prog[e]) for e in S.ENGS}, nwait=S.nwait, peak=A.peak,
                    gla_peak=gla_peak, na_peak=na_peak)
    return nc, info


def _consts():
    c = np.zeros((8, 128, 128), np.float32)
    c[0] = np.eye(128)
    c[1, :64, :64] = 1.0
    c[1, 64:, 64:] = 1.0
    j = np.arange(128)[:, None]
    i = np.arange(128)[None, :]
    c[2] = np.where(j <= i, -1.0 / 16, 0.0)
    c[3] = np.where(j > i, -1.0 / 16, 0.0)
    c[4] = np.where(j <= i, 1.0, 0.0)
    c[5] = np.where(j >= i, -1.0 / 16, 0.0)
    c[6] = np.where(j < i, -1.0 / 16, 0.0)
    c[7] = np.where(j >= i, 1.0, 0.0)
    return c


def _win_cols(j):
    o_gq, o_gk, o_gv, o_rf, o_rb, o_gg = 0, 512, 1024, 2048, 2064, 2080
    o_nq, o_nk, o_nv, o_ng = 3104, 4128, 5152, 6176
    r = lambda a, n: list(range(a, a + n))
    cols = []
    cols += r(o_gq + 128 * j, 128)
    cols += r(o_gk + 128 * j, 128)
    cols += r(o_nq + 256 * j, 256)
    cols += r(o_nk + 256 * j, 256)
    cols += r(o_rf, 16) + r(o_rb, 16)
    cols += r(o_gk + 128 * j, 128)
    cols += r(o_gv + 256 * j, 256)
    cols += r(o_nv + 256 * j, 256)
    cols += r(o_gg + 256 * j, 256)
    cols += r(o_ng + 256 * j, 256)
    assert len(cols) == NCOL
    return np.array(cols)


def _bias_tiles(rpb_h4, nrows):
    W = GRID_W
    ntile = nrows // 2
    c = np.arange(W)
    cs = np.clip(c - 8, 0, W - 16)
    jp = np.arange(W)[:, None]
    cc = c[None, :]
    colvalid = (jp >= cs[None, :]) & (jp < cs[None, :] + 16)
    colidx = np.clip(jp - cc + 15, 0, 30)

    def block(h, kr, qr):
        rs = min(max(qr - 4, 0), nrows - 8)
        if not (rs <= kr <= rs + 7):
            return np.full((W, W), MASKV, np.float32)
        d = kr - qr + 7
        blk = rpb_h4[h, d][colidx]
        return np.where(colvalid, blk, np.float32(MASKV)).astype(np.float32)

    def tile(h, t, m):
        out = np.empty((128, 128), np.float32)
        for a in range(2):
            for b in range(2):
                out[a * 64:(a + 1) * 64, b * 64:(b + 1) * 64] = block(h, 2 * t + a, 2 * m + b)
        return out

    tiles = np.empty((N_BT, 128, 128), np.float32)
    for h in range(4):
        for D in range(-2, 3):
            tiles[h * 5 + D + 2] = tile(h, 3 + D, 3)
    for e, m in enumerate((0, 1, ntile - 2, ntile - 1)):
        t0 = 0 if m < 2 else ntile - 4
        for tt in range(4):
            for h in range(4):
                tiles[20 + (e * 4 + h) * 4 + tt] = tile(h, t0 + tt, m)
    return tiles


def _prep_inputs(inp):
    x = np.asarray(inp["x"], np.float32)
    w_in = np.asarray(inp["w_in"], np.float32)[0]
    w_out = np.asarray(inp["w_out"], np.float32)[0]
    consts = _consts()
    gtile = np.ascontiguousarray(np.broadcast_to(np.asarray(inp["norm_g"], np.float32)[0][None, :], (128, D_MODEL)))
    gout = np.ascontiguousarray(np.broadcast_to(np.asarray(inp["gla_out_norm_g"], np.float32)[0][None, :], (128, 256)))
    qkg = np.stack([np.tile(np.asarray(inp["q_norm_g"], np.float32)[0], 2),
                    np.tile(np.asarray(inp["k_norm_g"], np.float32)[0], 2)], axis=1)
    seq = x.shape[1]
    perm = []
    for j in range(4):
        perm += list(range(256 * j, 256 * j + 256)) + list(range(1024 + 256 * j, 1024 + 256 * j + 256))
    w_out_p = w_out
    meta = np.ascontiguousarray(np.asarray(inp["meta_tokens"], np.float32))
    wdf = np.asarray(inp["w_decay_fwd"], np.float32)[0]
    wdb = np.asarray(inp["w_decay_bwd"], np.float32)[0]
    bdf = np.asarray(inp["b_decay_fwd"], np.float32)[0]
    bdb = np.asarray(inp["b_decay_bwd"], np.float32)[0]
    rpb = np.asarray(inp["rpb"], np.float32)[0]
    mbias = np.asarray(inp["meta_bias"], np.float32)[0]
    in_maps = []
    for c in range(8):
        b, j = c // 4, c % 4
        wdec = np.zeros((17, 256), np.float32)
        wdec[:16, :128] = wdf[:, 128 * j:128 * j + 128]
        wdec[:16, 128:] = wdb[:, 128 * j:128 * j + 128]
        wdec[16, :128] = bdf[128 * j:128 * j + 128]
        wdec[16, 128:] = bdb[128 * j:128 * j + 128]
        in_maps.append({
            "x": np.ascontiguousarray(x[b]),
            "meta": meta,
            "gtile": gtile,
            "w_in": np.ascontiguousarray(w_in[:, _win_cols(j)]),
            "wdec": wdec,
            "gout": gout,
            "qkg": np.ascontiguousarray(qkg),
            "biast": _bias_tiles(rpb[4 * j:4 * j + 4], seq // GRID_W),
            "metab": np.ascontiguousarray(mbias[4 * j:4 * j + 4].T),
            "w_out": np.ascontiguousarray(w_out_p[:, 256 * j:256 * j + 256]),
            "consts": consts,
            "xres": np.ascontiguousarray(x[b, :, 256 * j:256 * j + 256].T),
        })
    return in_maps


_CACHE = {}


def kernel(**inputs):
    debug = int(inputs.pop("_debug", 0))
    seq = int(np.asarray(inputs["x"]).shape[1])
    key = (seq, debug)
    if key not in _CACHE:
        _CACHE[key] = build_program(seq, debug)
    nc, info = _CACHE[key]
    in_maps = _prep_inputs(inputs)
    res = run_bass_kernel_spmd(nc, in_maps, core_ids=list(range(8)))
    out = np.empty((2, seq, D_MODEL), np.float32)
    for c in range(8):
        b, j = c // 4, c % 4
        out[b, :, 256 * j:256 * j + 256] = np.asarray(res.results[c]["out"]).T
    if debug:
        return out, res, info
    return out
```

```python
import numpy as np
from contextlib import ExitStack
import concourse.bass as bass
import concourse.mybir as mybir
from concourse.bass_utils import run_bass_kernel_spmd

F32 = mybir.dt.float32
BF16 = mybir.dt.bfloat16
AF = mybir.ActivationFunctionType
ALU = mybir.AluOpType

D_MODEL = 1024
SEQ = 8192
N_META = 16
L_TOK = SEQ + N_META
GRID_W = 64
N_ROWS = SEQ // GRID_W
EPS = 1e-6
NFM = 800
NTM = 1152
NCOL = NFM + NTM
MASKV = -1.0e4
NTILE = 64
N_BT = 84


class Sched:
    ENGS = ("pe", "act", "dve", "pool", "sp")
    NDMA = {"sp": 32, "pool": 16, "act": 16}

    def __init__(self, nc, stack):
        self.nc = nc
        self.prog = {e: [] for e in self.ENGS}
        self.esem = {e: stack.enter_context(nc.semaphore("s_" + e)) for e in self.ENGS}
        self.dsem = {q: [stack.enter_context(nc.semaphore("d_%s%d" % (q, i))) for i in range(n)]
                     for q, n in self.NDMA.items()}
        self.ccsem = stack.enter_context(nc.semaphore("s_cc"))
        self.cccnt = 0
        self.cclast = None
        self.cnt = {e: 0 for e in self.ENGS}
        self.dcnt = {q: 0 for q in self.NDMA}
        self.dlast = {}
        self.known = {e: {} for e in self.ENGS}
        self.buf = {}
        self.nwait = 0

    def _wait(self, eng, tok):
        if tok is None:
            return
        sem, val = tok
        k = self.known[eng]
        if k.get(id(sem), 0) >= val:
            return
        k[id(sem)] = val
        self.prog[eng].append(("w", sem, val))
        self.nwait += 1

    def _deps(self, eng, reads, writes):
        for key in reads:
            st = self.buf.get(key)
            if st is not None:
                self._waitdep(eng, st[0])
        for key in writes:
            st = self.buf.get(key)
            if st is not None:
                self._waitdep(eng, st[0])
                for t in st[1]:
                    self._waitdep(eng, t)

    def _waitdep(self, eng, tok):
        if tok is None:
            return
        if eng == "pe" and tok[0] is self.esem["pe"]:
            return
        self._wait(eng, tok)

    def _mark(self, tok, reads, writes):
        for key in reads:
            st = self.buf.setdefault(key, [None, []])
            st[1].append(tok)
        for key in writes:
            self.buf[key] = [tok, []]

    @staticmethod
    def _split(r, w):
        r2 = [k for k in r if not (isinstance(k, tuple) and k[0] == "ps")]
        w2 = list(w) + [k for k in r if isinstance(k, tuple) and k[0] == "ps"]
        return r2, w2

    def op(self, eng, name, kw, r=(), w=(), inc=True):
        r, w = self._split(r, w)
        self._deps(eng, r, w)
        if inc:
            self.cnt[eng] += 1
            tok = (self.esem[eng], self.cnt[eng])
            self.prog[eng].append(("o", name, kw, self.esem[eng], 1))
        else:
            assert eng == "pe"
            tok = (self.esem[eng], self.cnt[eng] + 1)
            self.prog[eng].append(("o", name, kw, None, 0))
        self._mark(tok, r, w)
        return tok

    def dma(self, q, out, in_, r=(), w=()):
        self._deps(q, r, w)
        n = self.NDMA[q]
        k = self.dcnt[q]
        self.dcnt[q] += 1
        sem = self.dsem[q][k % n]
        if k >= n:
            self._wait(q, (sem, 16 * (k // n)))
        tok = (sem, 16 * (k // n + 1))
        self.prog[q].append(("o", "dma_start", dict(out=out, in_=in_), sem, 16))
        self.dlast[id(sem)] = tok
        self._mark(tok, r, w)
        return tok

    def collective(self, kw, r=(), w=()):
        self._deps("pool", list(r), list(w))
        self.cccnt += 1
        tok = (self.ccsem, self.cccnt)
        self.prog["pool"].append(("o", "collective_compute", kw, self.ccsem, 1))
        self.cclast = tok
        self._mark(tok, list(r), list(w))
        return tok

    def barrier(self, with_cc=False):
        toks = [(self.esem[e], self.cnt[e]) for e in self.ENGS if self.cnt[e] > 0]
        toks += list(self.dlast.values())
        if with_cc and self.cclast is not None:
            toks.append(self.cclast)
        for e in self.ENGS:
            for t in toks:
                self._wait(e, t)
        self.buf = {}

    def emit(self, eng, e):
        for item in self.prog[eng]:
            if item[0] == "w":
                e.wait_ge(item[1], item[2])
            else:
                _, name, kw, sem, inc = item
                ins = getattr(e, name)(**kw)
                if sem is not None:
                    ins.then_inc(sem, inc)


class Arena:
    def __init__(self, t32, nbytes):
        self.t32 = t32
        self.tb = t32.bitcast(BF16)
        self.nbytes = nbytes
        self.off = 0
        self.peak = 0

    def alloc(self, free_shape, dt):
        esz = 4 if dt == F32 else 2
        n = int(np.prod(free_shape))
        self.off = (self.off + 63) // 64 * 64
        o = self.off
        self.off += n * esz
        self.peak = max(self.peak, self.off)
        assert self.off <= self.nbytes, ("SBUF arena overflow", self.off, self.nbytes)
        base = self.t32 if dt == F32 else self.tb
        ap = base[:, o // esz: o // esz + n]
        if len(free_shape) == 2:
            ap = ap.rearrange("p (a b) -> p a b", a=free_shape[0])
        elif len(free_shape) == 3:
            ap = ap.rearrange("p (a b c) -> p a b c", a=free_shape[0], b=free_shape[1])
        return ap


def build_program(seq=8192, debug=0):
    nc = bass.Bass("TRN2", target_bir_lowering=False)
    L = seq + N_META
    ntile = seq // 128
    nrows = seq // GRID_W

    def din(name, shape):
        return nc.dram_tensor(name, list(shape), F32, kind="ExternalInput")

    x_d = din("x", [seq, D_MODEL])
    meta_d = din("meta", [N_META, D_MODEL])
    gt_d = din("gtile", [128, D_MODEL])
    win_d = din("w_in", [D_MODEL, NCOL])
    wdec_d = din("wdec", [17, 256])
    gout_d = din("gout", [128, 256])
    qkg_d = din("qkg", [128, 2])
    bias_d = din("biast", [N_BT, 128, 128])
    mb_d = din("metab", [N_META, 4])
    wout_d = din("w_out", [2048, 256])
    cst_d = din("consts", [8, 128, 128])
    xres_d = din("xres", [256, seq])
    out_d = nc.dram_tensor("out", [256, seq], F32, kind="ExternalOutput")

    st_qk = nc.dram_tensor("st_qk", [128, 2, seq], BF16)
    st_rT = nc.dram_tensor("st_rT", [32, L], F32)
    st_nqT = nc.dram_tensor("st_nqT", [256, seq], BF16)
    st_nkT = nc.dram_tensor("st_nkT", [256, L], BF16)
    st_tm = nc.dram_tensor("st_tm", [L, NTM], BF16)
    XC = 512
    nxc = seq // XC
    exg_in_l = [nc.dram_tensor("exg_in%d" % k, [256, XC], BF16) for k in range(nxc)]
    exn_in_l = [nc.dram_tensor("exn_in%d" % k, [256, XC], BF16) for k in range(nxc)]
    exg_out_l = [nc.dram_tensor("exg_out%d" % k, [1024, XC], BF16) for k in range(nxc)]
    exn_out_l = [nc.dram_tensor("exn_out%d" % k, [1024, XC], BF16) for k in range(nxc)]
    ccg, ccn = {}, {}
    RG = [[0, 1, 2, 3], [4, 5, 6, 7]]

    def ex_in_ap(r0, r1, t0, n):
        k, o = t0 // XC, t0 % XC
        assert (r0, r1) in ((0, 256), (256, 512))
        return (exg_in_l if r0 == 0 else exn_in_l)[k][0:256, o:o + n]
    if debug:
        dbg_ex = nc.dram_tensor("dbg_ex", [512, seq], BF16, kind="ExternalOutput")

    ARENA_BYTES = 190 * 1024
    with ExitStack() as stack:
        arena_t = stack.enter_context(nc.sbuf_tensor("arena", [128, ARENA_BYTES // 4], F32))
        banks = [stack.enter_context(nc.psum_tensor("bank%d" % i, [128, 512], F32)) for i in range(8)]
        banksb = [b.bitcast(BF16) for b in banks]
        S = Sched(nc, stack)
        S_real = S
        A = Arena(arena_t, ARENA_BYTES)

        class Rec:
            def __init__(self):
                self.l = []

            def op(self, *a, **k):
                self.l.append(("op", a, k))

            def dma(self, *a, **k):
                self.l.append(("dma", a, k))

        def interleave(thunks):
            nonlocal S
            lists = []
            for th in thunks:
                rec = Rec()
                S = rec
                try:
                    th()
                finally:
                    S = S_real
                if rec.l:
                    lists.append(rec.l)
            pos = [0] * len(lists)
            live = True
            while live:
                live = False
                for li, l in enumerate(lists):
                    if pos[li] < len(l):
                        kind, a, k = l[pos[li]]
                        pos[li] += 1
                        getattr(S_real, kind)(*a, **k)
                        live = True

        ident = A.alloc([128], BF16)
        onesblk = A.alloc([128], BF16)
        cstage = A.alloc([2, 128], F32)
        for i, dst in enumerate((ident, onesblk)):
            S.dma("sp", cstage[:, i, :], cst_d[i, :, :], w=[("cstage", i)])
            S.op("dve", "tensor_copy", dict(out=dst, in_=cstage[:, i, :]), r=[("cstage", i)], w=[("const", i)])
        Bm = A.alloc([N_BT, 128], BF16)
        bst = A.alloc([2, 4, 128], F32)
        bm_todo = list(range(N_BT // 4))

        def bm_step():
            if not bm_todo:
                return
            k = bm_todo.pop(0)
            s_ = k % 2
            S.dma("sp", bst[:, s_, :, :], bias_d[4 * k:4 * k + 4, :, :].rearrange("t p q -> p t q"), w=[("bst", s_)])
            S.op("act", "activation", dict(out=Bm[:, 4 * k:4 * k + 4, :], in_=bst[:, s_, :, :], func=AF.Exp),
                 r=[("bst", s_)], w=[("Bm", k)])
        Bm_keys = []
        persist_off = A.off

        Wb = A.alloc([8, NCOL], BF16)
        wst = A.alloc([4, NCOL], F32)
        gtile = A.alloc([D_MODEL], F32)
        qkg = A.alloc([2], F32)
        xt = A.alloc([8, D_MODEL], F32)
        sqj = A.alloc([D_MODEL], BF16)
        ssb = A.alloc([2, 4], F32)
        rmsb = A.alloc([2, 4], F32)
        u = A.alloc([4, D_MODEL], BF16)
        uT = A.alloc([2, 8, 512], BF16)
        fmst = A.alloc([3, 512], BF16)
        rst = A.alloc([2, 512], F32)
        sqn = A.alloc([2, 512], BF16)
        rmsn = A.alloc([2, 512], F32)
        tmst = A.alloc([2, NTM], BF16)

        S.dma("sp", gtile, gt_d[:, :], w=["gtile"])
        S.dma("sp", qkg, qkg_d[:, :], w=["qkg"])
        for kc in range(8):
            s = kc % 4
            S.dma("sp", wst[:, s, :], win_d[kc * 128:(kc + 1) * 128, :], w=[("wst", s)])
            h1 = NCOL // 2
            S.op("dve", "tensor_copy", dict(out=Wb[:, kc, 0:h1], in_=wst[:, s, 0:h1]), r=[("wst", s)], w=[("Wb", kc)])
            S.op("pool", "tensor_copy", dict(out=Wb[:, kc, h1:NCOL], in_=wst[:, s, h1:NCOL]), r=[("wst", s)], w=[("Wb", kc, 1)])

        FM_TILES = [(256, 128, "nq0"), (0, 128, "q"), (384, 128, "nq1"), (128, 128, "k"),
                    (512, 128, "nk0"), (768, 32, "r"), (640, 128, "nk1")]
        cn = dict(x=0, u=0, fm=0, r=0, n=0, tm=0, ptr=0, pfm=0, ptm=0)

        def p1_prep_a_items(gi, tiles):
            us = gi % 2
            ntl = len(tiles)
            xbase = cn["x"]
            items = []
            for ti, (src, nt, ltk, xtk) in enumerate(tiles):
                xs = cn["x"] % 8
                cn["x"] += 1
                S.dma("sp", xt[:nt, xs, :], src, w=[("xt", xs)])

                def sq(ti=ti, nt=nt, xs=xs):
                    S.op("act", "activation", dict(out=sqj[:nt, :], in_=xt[:nt, xs, :], func=AF.Square,
                                                   accum_out=ssb[:nt, us, ti:ti + 1]),
                         r=[("xt", xs)], w=["sqj", ("ssb", us, ti)])
                items.append(sq)
            nt0 = tiles[0][1]

            def rs():
                S.op("act", "activation", dict(out=rmsb[:nt0, us, :ntl], in_=ssb[:nt0, us, :ntl], func=AF.Ln,
                                               scale=1.0 / D_MODEL, bias=EPS),
                     r=[("ssb", us, ti) for ti in range(ntl)], w=[("rmsb", us)])
                S.op("act", "activation", dict(out=rmsb[:nt0, us, :ntl], in_=rmsb[:nt0, us, :ntl], func=AF.Exp, scale=-0.5),
                     r=[("rmsb", us)], w=[("rmsb", us)])
            items.append(rs)
            for ti, (src, nt, ltk, xtk) in enumerate(tiles):
                xs = (xbase + ti) % 8

                def uu(ti=ti, nt=nt, xs=xs):
                    S.op("dve", "scalar_tensor_tensor",
                         dict(out=u[:nt, ti, :], in0=xt[:nt, xs, :], scalar=rmsb[:nt, us, ti:ti + 1],
                              in1=gtile[:nt, :], op0=ALU.mult, op1=ALU.mult),
                         r=[("xt", xs), ("rmsb", us), "gtile"], w=[("u", ti)])
                items.append(uu)
            return items

        def p1_prep_b(gi, tiles, ti):
            us = gi % 2
            nt = tiles[ti][1]
            toff = sum(t[1] for t in tiles[:ti])
            pb = cn["ptr"] % 2
            cn["ptr"] += 1
            trv = banksb[pb][:, 0:1024].rearrange("p (a b) -> p a b", a=8)
            for kc in range(8):
                S.op("pe", "transpose", dict(out=trv[:, kc, :nt], in_=u[:nt, ti, kc * 128:(kc + 1) * 128],
                                             identity=ident[:nt, :nt]),
                     r=[("u", ti), ("const", 0)], w=[("ps", pb)], inc=(kc == 7))
            if ti % 2 == 0:
                S.op("act", "activation", dict(out=uT[:, us, :, toff:toff + nt], in_=trv[:, :, :nt], func=AF.Copy),
                     r=[("ps", pb)], w=[("uT", us, ti)])
            else:
                S.op("dve", "tensor_copy", dict(out=uT[:, us, :, toff:toff + nt], in_=trv[:, :, :nt]),
                     r=[("ps", pb)], w=[("uT", us, ti)])

        def p1_main(gi, tiles, hook):
            deferred = []
            T = sum(t[1] for t in tiles)
            us = gi % 2
            ntl = len(tiles)
            uT_keys = [("uT", us, ti) for ti in range(ntl)]
            ltok0 = tiles[0][2]
            xtok0 = tiles[0][3]
            for (c0, rows, kind) in FM_TILES:
                if xtok0 is None and kind in ("q", "k", "nq0", "nq1"):
                    continue
                pb = (2, 3, 7)[cn["pfm"] % 3]
                cn["pfm"] += 1
                for kc in range(8):
                    S.op("pe", "matmul", dict(out=banks[pb][:rows, :T], lhsT=Wb[:, kc, c0:c0 + rows],
                                              rhs=uT[:, us, kc, :T], start=(kc == 0), stop=(kc == 7)),
                         r=uT_keys + [("Wb", kc), ("Wb", kc, 1)], w=[("ps", pb)], inc=(kc == 7))
                while deferred:
                    deferred.pop(0)()
                hook()
                if kind in ("q", "k"):
                    fs = cn["fm"] % 3
                    cn["fm"] += 1
                    S.op("act", "activation", dict(out=fmst[:, fs, :T], in_=banks[pb][:, :T], func=AF.Copy),
                         r=[("ps", pb)], w=[("fmst", fs)])
                    dst = st_qk[:, 0 if kind == "q" else 1, :]
                    S.dma("pool", dst[:, xtok0:xtok0 + T], fmst[:, fs, :T], r=[("fmst", fs)])
                elif kind == "r":
                    rs_ = cn["r"] % 2
                    cn["r"] += 1
                    S.op("dve", "tensor_copy", dict(out=rst[:32, rs_, :T], in_=banks[pb][:32, :T]),
                         r=[("ps", pb)], w=[("rst", rs_)])
                    S.dma("pool", st_rT[:, ltok0:ltok0 + T], rst[:32, rs_, :T], r=[("rst", rs_)])
                else:
                    isq = kind.startswith("nq")
                    half = int(kind[-1])
                    ns = cn["n"] % 2
                    cn["n"] += 1
                    S.op("act", "activation", dict(out=sqn[:, ns, :T], in_=banks[pb][:, :T], func=AF.Square),
                         r=[("ps", pb)], w=[("sqn", ns)])
                    fs = cn["fm"] % 3
                    cn["fm"] += 1

                    def chain(pb=pb, ns=ns, fs=fs, isq=isq, half=half):
                        S.op("pe", "matmul", dict(out=banks[6][:, :T], lhsT=onesblk, rhs=sqn[:, ns, :T],
                                                  start=True, stop=True),
                             r=[("sqn", ns), ("const", 1)], w=[("ps", 6)])
                        S.op("act", "activation", dict(out=rmsn[:, ns, :T], in_=banks[6][:, :T], func=AF.Ln,
                                                       scale=1.0 / 64, bias=EPS),
                             r=[("ps", 6)], w=[("rmsn", ns)])
                        S.op("act", "activation", dict(out=rmsn[:, ns, :T], in_=rmsn[:, ns, :T], func=AF.Exp, scale=-0.5),
                             r=[("rmsn", ns)], w=[("rmsn", ns)])
                        gcol = 0 if isq else 1
                        S.op("dve", "scalar_tensor_tensor",
                             dict(out=fmst[:, fs, :T], in0=banks[pb][:, :T], scalar=qkg[:, gcol:gcol + 1],
                                  in1=rmsn[:, ns, :T], op0=ALU.mult, op1=ALU.mult),
                             r=[("ps", pb), ("rmsn", ns), "qkg"], w=[("fmst", fs)])
                        if isq:
                            S.dma("pool", st_nqT[half * 128:(half + 1) * 128, xtok0:xtok0 + T], fmst[:, fs, :T],
                                  r=[("fmst", fs)])
                        else:
                            S.dma("pool", st_nkT[half * 128:(half + 1) * 128, ltok0:ltok0 + T], fmst[:, fs, :T],
                                  r=[("fmst", fs)])
                    deferred.append(chain)
            toff = 0
            for ti, (src, nt, ltk, xtk) in enumerate(tiles):
                ts_ = cn["tm"] % 2
                cn["tm"] += 1
                for n0 in (0, 384, 768):
                    nsz = 384
                    pb = 4 + cn["ptm"] % 2
                    cn["ptm"] += 1
                    for kc in range(8):
                        S.op("pe", "matmul", dict(out=banks[pb][:nt, :nsz], lhsT=uT[:, us, kc, toff:toff + nt],
                                                  rhs=Wb[:, kc, NFM + n0:NFM + n0 + nsz], start=(kc == 0), stop=(kc == 7)),
                             r=[("uT", us, ti), ("Wb", kc), ("Wb", kc, 1)], w=[("ps", pb)], inc=(kc == 7))
                    while deferred:
                        deferred.pop(0)()
                    plain_hi = min(max(640 - n0, 0), nsz)
                    if plain_hi > 0:
                        S.op("dve", "tensor_copy", dict(out=tmst[:nt, ts_, n0:n0 + plain_hi], in_=banks[pb][:nt, 0:plain_hi]),
                             r=[("ps", pb)], w=[("tmst", ts_, n0, 0)])
                    if plain_hi < nsz:
                        S.op("act", "activation", dict(out=tmst[:nt, ts_, n0 + plain_hi:n0 + nsz], in_=banks[pb][:nt, plain_hi:nsz],
                                                       func=AF.Silu),
                             r=[("ps", pb)], w=[("tmst", ts_, n0, 1)])
                hook()
                S.dma("pool", st_tm[ltk:ltk + nt, :], tmst[:nt, ts_, :],
                      r=[("tmst", ts_, 0, 0), ("tmst", ts_, 384, 0), ("tmst", ts_, 384, 1), ("tmst", ts_, 768, 1)])
                toff += nt

        groups = [[(meta_d[:, :], N_META, 0, None)]]
        for g in range(seq // 512):
            groups.append([(x_d[g * 512 + i * 128: g * 512 + (i + 1) * 128, :], 128,
                            N_META + g * 512 + i * 128, g * 512 + i * 128) for i in range(4)])
        for it in p1_prep_a_items(0, groups[0]):
            it()
        for ti in range(len(groups[0])):
            p1_prep_b(0, groups[0], ti)
        for gi in range(len(groups)):
            pending = []
            if gi + 1 < len(groups):
                pending = p1_prep_a_items(gi + 1, groups[gi + 1])
                pending += [(lambda g2=gi + 1, t2=ti: p1_prep_b(g2, groups[g2], t2)) for ti in range(len(groups[gi + 1]))]

            def hook():
                if pending:
                    pending.pop(0)()
            bm_step()
            bm_step()
            p1_main(gi, groups[gi], hook)
            while pending:
                pending.pop(0)()
        while bm_todo:
            bm_step()
        S.barrier()

        A.off = persist_off
        NL = 7
        NP = 3
        tri = A.alloc([6, 128], F32)
        wdec = A.alloc([256], F32)
        gout = A.alloc([256], F32)
        ofall = A.alloc([ntile, 256], F32)
        S32 = A.alloc([2, 256], F32)
        Sb = A.alloc([2, 256], BF16)
        qkT = A.alloc([2, NL, 256], BF16)
        raug = A.alloc([2, NL, 128], F32)
        tmkv = A.alloc([2, NL, 896], BF16)
        r_hi = A.alloc([2, 2, 128], BF16)
        r_lo = A.alloc([2, 2, 128], BF16)
        e1 = A.alloc([2, 2, 128], F32)
        lsp = A.alloc([2, 2, 128], F32)
        l_hi = A.alloc([2, NP, 128], BF16)
        l_lo = A.alloc([2, NP, 128], BF16)
        EG = A.alloc([2, NP, 128], F32)
        EnG = A.alloc([2, 2, 128], F32)
        Est = A.alloc([2, 2, 128], F32)
        kin = A.alloc([2, 2, 128], BF16)
        kst = A.alloc([2, 2, 128], BF16)
        qin = A.alloc([2, NP, 128], BF16)
        ATm = A.alloc([2, NP, 128], BF16)
        Usb = A.alloc([2, NP, 256], F32)
        osum = A.alloc([2, 2, 256], F32)
        gsg = A.alloc([2, 2, 256], F32)
        ogb = A.alloc([2, 2, 256], BF16)
        oTs = A.alloc([2, 2, 256], BF16)
        sqj2 = A.alloc([256], BF16)
        ssq = A.alloc([2, 2, 1], F32)
        lnv = A.alloc([2, 2, 1], F32)
        rstd = A.alloc([2, 2, 1], F32)
        trib = A.alloc([6, 128], BF16)
        wd_hi = A.alloc([256], BF16)
        wd_lo = A.alloc([256], BF16)

        S.dma("sp", tri, cst_d[2:8, :, :].rearrange("t p q -> p t q"), w=["tri"])
        S.dma("sp", wdec[:17, :], wdec_d[:, :], w=["wdec"])
        S.dma("sp", gout, gout_d[:, :], w=["gout"])
        S.op("dve", "tensor_copy", dict(out=trib, in_=tri), r=["tri"], w=["trib"])
        S.op("dve", "tensor_copy", dict(out=wd_hi[:17, :], in_=wdec[:17, :]), r=["wdec"], w=["wd_hi"])
        S.op("dve", "tensor_tensor", dict(out=wd_lo[:17, :], in0=wdec[:17, :], in1=wd_hi[:17, :], op=ALU.subtract),
             r=["wdec", "wd_hi"], w=["wd_lo"])
        for d in range(2):
            for sl in range(NL):
                S.op("pool", "memset", dict(ap=raug[:17, d, sl, :], constant=1.0), w=[("raug", d, sl)])
        S.op("pool", "memset", dict(ap=S32[:, 1, :], constant=0.0), w=[("S32", 1)])
        S.op("pool", "memset", dict(ap=Sb[:, 1, :], constant=0.0), w=[("Sb", 1)])

        LG = (0, 4)
        GG = (1, 5)
        AU = (2, 6)
        OB = (3, 7)
        QSCALE = 128 ** -0.5

        def logits_mm(d, ts, ncol, rsrc, rkeys, outreg, outkey):
            S.op("dve", "tensor_copy", dict(out=r_hi[:17, d, ts, :ncol], in_=rsrc), r=rkeys, w=[("r_hi", d, ts)])
            S.op("dve", "tensor_tensor", dict(out=r_lo[:17, d, ts, :ncol], in0=rsrc, in1=r_hi[:17, d, ts, :ncol],
                                              op=ALU.subtract), r=rkeys + [("r_hi", d, ts)], w=[("r_lo", d, ts)])
            wsl = slice(128 * d, 128 * d + 128)
            S.op("pe", "matmul", dict(out=outreg, lhsT=r_hi[:17, d, ts, :ncol], rhs=wd_hi[:17, wsl], start=True, stop=False),
                 r=[("r_hi", d, ts), "wd_hi"], w=[outkey])
            S.op("pe", "matmul", dict(out=outreg, lhsT=r_lo[:17, d, ts, :ncol], rhs=wd_hi[:17, wsl], start=False, stop=False),
                 r=[("r_lo", d, ts), "wd_hi"], w=[outkey])
            S.op("pe", "matmul", dict(out=outreg, lhsT=r_hi[:17, d, ts, :ncol], rhs=wd_lo[:17, wsl], start=False, stop=True),
                 r=[("r_hi", d, ts), "wd_lo"], w=[outkey])

        def softplus_split(d, ts, ps_, npart, lgreg, lgkey):
            S.op("act", "activation", dict(out=e1[:npart, d, ts, :], in_=lgreg, func=AF.Exp, scale=-1.0),
                 r=[lgkey], w=[("e1", d, ts)])
            S.op("act", "activation", dict(out=lsp[:npart, d, ts, :], in_=e1[:npart, d, ts, :], func=AF.Ln, bias=1.0),
                 r=[("e1", d, ts)], w=[("lsp", d, ts)])
            S.op("dve", "tensor_copy", dict(out=l_hi[:npart, d, ps_, :], in_=lsp[:npart, d, ts, :]),
                 r=[("lsp", d, ts)], w=[("l_hi", d, ps_)])
            S.op("dve", "tensor_tensor", dict(out=l_lo[:npart, d, ps_, :], in0=lsp[:npart, d, ts, :],
                                              in1=l_hi[:npart, d, ps_, :], op=ALU.subtract),
                 r=[("lsp", d, ts), ("l_hi", d, ps_)], w=[("l_lo", d, ps_)])

        S.dma("sp", raug[:16, 0, 0, 0:16], st_rT[0:16, 0:16], w=[("raug", 0, 0)])
        S.dma("sp", tmkv[:16, 0, 0, 0:384], st_tm[0:16, 0:384], w=[("tmkv", 0, 0)])
        logits_mm(0, 0, 16, raug[:17, 0, 0, 0:16], [("raug", 0, 0)], banks[LG[0]][:16, 0:128], ("ps", LG[0]))
        softplus_split(0, 0, 0, 16, banks[LG[0]][:16, 0:128], ("ps", LG[0]))
        S.op("pe", "matmul", dict(out=banks[GG[0]][:16, 0:128], lhsT=trib[:16, 1, 0:16], rhs=l_hi[:16, 0, 0, :],
                                  start=True, stop=False), r=[("l_hi", 0, 0), "trib"], w=[("ps", GG[0])], inc=False)
        S.op("pe", "matmul", dict(out=banks[GG[0]][:16, 0:128], lhsT=trib[:16, 1, 0:16], rhs=l_lo[:16, 0, 0, :],
                                  start=False, stop=True), r=[("l_lo", 0, 0), "trib"], w=[("ps", GG[0])])
        S.op("act", "activation", dict(out=Est[:16, 0, 0, :], in_=banks[GG[0]][:16, 0:128], func=AF.Exp),
             r=[("ps", GG[0])], w=[("Est", 0, 0)])
        S.op("dve", "tensor_tensor", dict(out=kst[:16, 0, 0, :], in0=tmkv[:16, 0, 0, 0:128], in1=Est[:16, 0, 0, :],
                                          op=ALU.mult), r=[("tmkv", 0, 0), ("Est", 0, 0)], w=[("kst", 0, 0)])
        S.op("pe", "matmul", dict(out=banks[AU[0]][:, 128:384], lhsT=kst[:16, 0, 0, :], rhs=tmkv[:16, 0, 0, 128:384],
                                  start=True, stop=True), r=[("kst", 0, 0), ("tmkv", 0, 0)], w=[("ps", AU[0])])
        S.op("dve", "tensor_copy", dict(out=S32[:, 0, :], in_=banks[AU[0]][:, 128:384]), r=[("ps", AU[0])], w=[("S32", 0)])
        S.op("act", "activation", dict(out=Sb[:, 0, :], in_=banks[AU[0]][:, 128:384], func=AF.Copy),
             r=[("ps", AU[0])], w=[("Sb", 0)])

        gdone = {}
        do_cc = debug not in (3, 4)

        def chunk_of(d, i):
            return i if d == 0 else ntile - 1 - i

        def is_second(i):
            return i >= ntile // 2

        def stage_L(d, i):
            c = chunk_of(d, i)
            sl = i % NL
            lt = N_META + c * 128
            xt0 = c * 128
            S.dma("sp", raug[:16, d, sl, :], st_rT[16 * d:16 * d + 16, lt:lt + 128], w=[("raug", d, sl)])
            S.dma("sp", qkT[:, d, sl, :].rearrange("p (a b) -> p a b", a=2), st_qk[:, :, xt0:xt0 + 128],
                  w=[("qTt", d, sl), ("kTt", d, sl)])
            S.dma("sp", tmkv[:, d, sl, :], st_tm[lt:lt + 128, 0:896], w=[("tmkv", d, sl)])

        def stage_A1(d, i):
            sl, ts = i % NL, i % 2
            logits_mm(d, ts, 128, raug[:17, d, sl, :], [("raug", d, sl)], banks[LG[d]][:, 128 * ts:128 * ts + 128], ("ps", LG[d]))

        def stage_A2(d, i):
            ts, ps_ = i % 2, i % NP
            softplus_split(d, ts, ps_, 128, banks[LG[d]][:, 128 * ts:128 * ts + 128], ("ps", LG[d]))

        def stage_B1(d, i):
            sl, ts, ps_ = i % NL, i % 2, i % NP
            gg, au = GG[d], AU[d]
            S.op("pe", "matmul", dict(out=banks[gg][:, 0:128], lhsT=trib[:, 3 * d + 1, :], rhs=l_hi[:, d, ps_, :],
                                      start=True, stop=False), r=[("l_hi", d, ps_), "trib"], w=[("ps", gg)], inc=False)
            S.op("pe", "matmul", dict(out=banks[gg][:, 0:128], lhsT=trib[:, 3 * d + 1, :], rhs=l_lo[:, d, ps_, :],
                                      start=False, stop=True), r=[("l_lo", d, ps_), "trib"], w=[("ps", gg)], inc=False)
            S.op("pe", "matmul", dict(out=banks[gg][:, 128:256], lhsT=l_hi[:, d, ps_, :], rhs=trib[:, 3 * d, :],
                                      start=True, stop=False), r=[("l_hi", d, ps_), "trib"], w=[("ps", gg)], inc=False)
            S.op("pe", "matmul", dict(out=banks[gg][:, 128:256], lhsT=l_lo[:, d, ps_, :], rhs=trib[:, 3 * d, :],
                                      start=False, stop=True), r=[("l_lo", d, ps_), "trib"], w=[("ps", gg)])
            S.op("act", "activation", dict(out=EG[:, d, ps_, :], in_=banks[gg][:, 128:256], func=AF.Exp),
                 r=[("ps", gg)], w=[("EG", d, ps_)])
            S.op("act", "activation", dict(out=EnG[:, d, ts, :], in_=banks[gg][:, 128:256], func=AF.Exp, scale=-1.0),
                 r=[("ps", gg)], w=[("EnG", d, ts)])
            S.op("act", "activation", dict(out=Est[:, d, ts, :], in_=banks[gg][:, 0:128], func=AF.Exp),
                 r=[("ps", gg)], w=[("Est", d, ts)])

        def stage_B2(d, i):
            sl, ts, ps_ = i % NL, i % 2, i % NP
            gg, au = GG[d], AU[d]
            S.op("dve", "scalar_tensor_tensor", dict(out=qin[:, d, ps_, :], in0=qkT[:, d, sl, 0:128], scalar=QSCALE,
                                                     in1=EG[:, d, ps_, :], op0=ALU.mult, op1=ALU.mult),
                 r=[("qTt", d, sl), ("EG", d, ps_)], w=[("qin", d, ps_)])
            S.op("pool", "tensor_tensor", dict(out=kin[:, d, ts, :], in0=qkT[:, d, sl, 128:256], in1=EnG[:, d, ts, :], op=ALU.mult),
                 r=[("kTt", d, sl), ("EnG", d, ts)], w=[("kin", d, ts)])
            S.op("pool", "tensor_tensor", dict(out=kst[:, d, ts, :], in0=tmkv[:, d, sl, 0:128], in1=Est[:, d, ts, :], op=ALU.mult),
                 r=[("tmkv", d, sl), ("Est", d, ts)], w=[("kst", d, ts)])
            S.op("pe", "matmul", dict(out=banks[au][:, 0:128], lhsT=kin[:, d, ts, :], rhs=qin[:, d, ps_, :],
                                      start=True, stop=True),
                 r=[("kin", d, ts), ("qin", d, ps_)], w=[("ps", au)], inc=False)
            S.op("pe", "matmul", dict(out=banks[au][:, 128:384], lhsT=kst[:, d, ts, :], rhs=tmkv[:, d, sl, 128:384],
                                      start=True, stop=True), r=[("kst", d, ts), ("tmkv", d, sl)], w=[("ps", au)])
            S.op("dve", "tensor_tensor", dict(out=ATm[:, d, ps_, :], in0=banks[au][:, 0:128], in1=trib[:, 3 * d + 2, :],
                                              op=ALU.mult), r=[("ps", au), "trib"], w=[("ATm", d, ps_)])
            S.op("act", "activation", dict(out=Usb[:, d, ps_, :], in_=banks[au][:, 128:384], func=AF.Copy),
                 r=[("ps", au)], w=[("Usb", d, ps_)])

        def stage_C(d, i):
            c = chunk_of(d, i)
            sl, ts, ps_ = i % NL, i % 2, i % NP
            ob = OB[d]
            last = 127 if d == 0 else 0
            S.op("pe", "matmul", dict(out=banks[ob][:, 0:256], lhsT=ATm[:, d, ps_, :], rhs=tmkv[:, d, sl, 128:384],
                                      start=True, stop=False),
                 r=[("ATm", d, ps_), ("tmkv", d, sl)], w=[("ps", ob)], inc=False)
            S.op("pe", "matmul", dict(out=banks[ob][:, 0:256], lhsT=qin[:, d, ps_, :], rhs=Sb[:, d, :], start=False, stop=True),
                 r=[("qin", d, ps_), ("Sb", d)], w=[("ps", ob)])
            S.op("dve", "scalar_tensor_tensor", dict(out=S32[:, d, :], in0=S32[:, d, :], scalar=EG[:, d, ps_, last:last + 1],
                                                     in1=Usb[:, d, ps_, :], op0=ALU.mult, op1=ALU.add),
                 r=[("S32", d), ("EG", d, ps_), ("Usb", d, ps_)], w=[("S32", d)])
            S.op("pool", "tensor_copy", dict(out=Sb[:, d, :], in_=S32[:, d, :]), r=[("S32", d)], w=[("Sb", d)])
            if not is_second(i):
                S.op("act", "activation", dict(out=ofall[:, c, :], in_=banks[ob][:, 0:256], func=AF.Copy),
                     r=[("ps", ob)], w=[("ofall", c)])
            else:
                S.op("dve", "tensor_tensor", dict(out=osum[:, d, ts, :], in0=banks[ob][:, 0:256], in1=ofall[:, c, :], op=ALU.add),
                     r=[("ps", ob), ("ofall", c)], w=[("osum", d, ts)])

        def stage_D(d, i):
            if not is_second(i):
                return
            c = chunk_of(d, i)
            sl, ts = i % NL, i % 2
            ob = OB[d]
            xt0 = c * 128
            S.op("act", "activation", dict(out=sqj2, in_=osum[:, d, ts, :], func=AF.Square, accum_out=ssq[:, d, ts, :]),
                 r=[("osum", d, ts)], w=["sqj2", ("ssq", d, ts)])
            S.op("act", "activation", dict(out=lnv[:, d, ts, :], in_=ssq[:, d, ts, :], func=AF.Ln, scale=1.0 / 256, bias=EPS),
                 r=[("ssq", d, ts)], w=[("lnv", d, ts)])
            S.op("act", "activation", dict(out=rstd[:, d, ts, :], in_=lnv[:, d, ts, :], func=AF.Exp, scale=-0.5),
                 r=[("lnv", d, ts)], w=[("rstd", d, ts)])
            S.op("pool", "tensor_tensor", dict(out=gsg[:, d, ts, :], in0=gout, in1=tmkv[:, d, sl, 640:896], op=ALU.mult),
                 r=["gout", ("tmkv", d, sl)], w=[("gsg", d, ts)])
            S.op("dve", "scalar_tensor_tensor", dict(out=ogb[:, d, ts, :], in0=osum[:, d, ts, :], scalar=rstd[:, d, ts, :],
                                                     in1=gsg[:, d, ts, :], op0=ALU.mult, op1=ALU.mult),
                 r=[("osum", d, ts), ("rstd", d, ts), ("gsg", d, ts)], w=[("ogb", d, ts)])

        def stage_D2(d, i):
            if not is_second(i):
                return
            c = chunk_of(d, i)
            ts = i % 2
            xt0 = c * 128
            tb = LG[d]
            trv = banksb[tb][:, 512:768].rearrange("p (a b) -> p a b", a=2)
            for hf in range(2):
                S.op("pe", "transpose", dict(out=trv[:, hf, :], in_=ogb[:, d, ts, hf * 128:(hf + 1) * 128], identity=ident),
                     r=[("ogb", d, ts), ("const", 0)], w=[("ps", tb)], inc=(hf == 1))
            oTv = oTs[:, d, ts, :].rearrange("p (a b) -> p a b", a=2)
            S.op("act", "activation", dict(out=oTv, in_=trv, func=AF.Copy), r=[("ps", tb)], w=[("oTs", d, ts)])
            S.dma("act", ex_in_ap(0, 256, xt0, 128).rearrange("(a p) t -> p a t", p=128), oTv,
                  r=[("oTs", d, ts)], w=[("exin", xt0 // XC, "g", c)])
            kch = xt0 // XC
            gdone[kch] = gdone.get(kch, 0) + 1
            if do_cc and gdone[kch] == XC // 128:
                keys = [("exin", kch, "g", cc_) for cc_ in range(kch * (XC // 128), (kch + 1) * (XC // 128))]
                if debug:
                    S.dma("pool", dbg_ex[0:256, kch * XC:(kch + 1) * XC], exg_in_l[kch][:, :], r=keys)
                ccg[kch] = S.collective(dict(kind="AllGather", op=ALU.bypass, replica_groups=RG,
                                             ins=[exg_in_l[kch].ap().opt()], outs=[exg_out_l[kch].ap().opt()]), r=keys)

        for t in range(-6, ntile):
            th = []
            for d in range(2):
                if 0 <= t + 6 < ntile:
                    th.append(lambda d=d, i=t + 6: stage_L(d, i))
            for d in range(2):
                if 0 <= t + 4 < ntile:
                    th.append(lambda d=d, i=t + 4: stage_A1(d, i))
            for d in range(2):
                if 0 <= t + 3 < ntile:
                    th.append(lambda d=d, i=t + 3: stage_A2(d, i))
            for d in range(2):
                if 0 <= t + 2 < ntile:
                    th.append(lambda d=d, i=t + 2: stage_B1(d, i))
            for d in range(2):
                if 0 <= t + 1 < ntile:
                    th.append(lambda d=d, i=t + 1: stage_B2(d, i))
            for d in range(2):
                if 0 <= t < ntile:
                    th.append(lambda d=d, i=t: (stage_C(d, i), stage_D(d, i)))
            interleave(th)
            for d in range(2):
                if 0 <= t - 1 < ntile:
                    stage_D2(d, t - 1)
        for d in range(2):
            stage_D2(d, ntile - 1)
        S.barrier()
        gla_peak = A.off
        na_peak = 0
        do_na = debug != 3
        do_cc = debug not in (3, 4)

        if do_na:
            A.off = persist_off
            RING = 16
            NQ = 7
            LA = 5
            mbias = A.alloc([4], F32)
            nkm2 = A.alloc([2, 128], BF16)
            vma = A.alloc([4, 65], BF16)
            nkr2 = A.alloc([RING, 2, 128], BF16)
            var_ = A.alloc([RING, 4, 65], BF16)
            nqb = A.alloc([NQ, 2, 256], BF16)
            sng = A.alloc([NQ, 256], BF16)
            Et = A.alloc([3, 6, 256], BF16)
            Etm = A.alloc([3, 5, 256], BF16)
            emb = A.alloc([4], F32)
            rec = A.alloc([2, 4, 1], F32)
            ona = A.alloc([2, 256], BF16)
            NOS = 8
            oT2 = A.alloc([NOS, 2, 128], BF16)
            NOT = 4
            OT = A.alloc([NOT, 16, 512], BF16)
            ot_loaded = set()

            def load_OT(g):
                if g in ot_loaded or g >= seq // 512:
                    return
                ot_loaded.add(g)
                kch = (g * 512) // XC
                o = (g * 512) % XC
                S._wait("sp", ccg[kch])
                S.dma("sp", OT[:, g % NOT, 0:8, :], exg_out_l[kch][:, o:o + 512].rearrange("(a p) t -> p a t", p=128),
                      w=[("OT", g % NOT)])
                S._wait("pool", ccn[kch])
                S.dma("pool", OT[:, g % NOT, 8:16, :], exn_out_l[kch][:, o:o + 512].rearrange("(a p) t -> p a t", p=128),
                      w=[("OTb", g % NOT)])

            S.dma("sp", mbias[:16, :], mb_d[:, :], w=["mbias"])
            S.op("pool", "memset", dict(ap=nkm2, constant=0.0), w=["nkm2"])
            S.dma("sp", nkm2[:, :, 0:16], st_nkT[:, 0:16].rearrange("(pr q) t -> q pr t", q=128), r=["nkm2"], w=["nkm2p"])
            S.op("pool", "memset", dict(ap=vma[:16, :, :], constant=1.0), w=["vma"])
            S.dma("sp", vma[:16, :, 0:64], st_tm[0:16, 384:640].rearrange("t (h e) -> t h e", h=4), w=["vma"])
            S.op("act", "activation", dict(out=emb[:16, :], in_=mbias[:16, :], func=AF.Exp), r=["mbias"], w=["emb"])
            for h in range(4):
                S.op("dve", "tensor_scalar", dict(out=vma[:16, h, :], in0=vma[:16, h, :], scalar1=emb[:16, h:h + 1],
                                                  scalar2=None, op0=ALU.mult), r=["vma", "emb"], w=["vma"])
            S.op("pool", "memset", dict(ap=var_[:, :, :, 64:65].rearrange("p r h e -> p (r h) e"), constant=1.0),
                 w=[("var", sl) for sl in range(RING)])
            S.op("dve", "memset", dict(ap=nqb, constant=0.0), w=[("nqb", sl) for sl in range(NQ)])

            loaded = set()

            def load_key_tile(t):
                if t in loaded:
                    return
                loaded.add(t)
                sl = t % RING
                lt = N_META + t * 128
                S.dma("pool", nkr2[:, sl, :, :], st_nkT[:, lt:lt + 128].rearrange("(pr q) t -> q pr t", q=128), w=[("nkr", sl)])
                S.dma("pool", var_[:, sl, :, 0:64], st_tm[lt:lt + 128, 384:640].rearrange("t (h e) -> t h e", h=4),
                      w=[("var", sl)])

            def rs_of(r):
                return min(max(r - 4, 0), nrows - 8)

            edge_ms = (0, 1, ntile - 2, ntile - 1)

            def tiles_of(m):
                t0 = rs_of(2 * m) // 2
                t1 = (rs_of(2 * m + 1) + 7) // 2
                tl = list(range(t0, t1 + 1))
                assert len(tl) <= 5
                return tl

            def na_L(m):
                ms = m % NQ
                for t in tiles_of(m):
                    load_key_tile(t)
                for pr in range(2):
                    for e in range(2):
                        h = 2 * pr + e
                        S.dma("sp", nqb[64 * e:64 * e + 64, ms, pr, 128 * e:128 * e + 128],
                              st_nqT[64 * h:64 * h + 64, m * 128:(m + 1) * 128], r=[("nqb", ms)], w=[("nqbp", ms, h)])
                S.dma("sp", sng[:, ms, :], st_tm[N_META + m * 128:N_META + (m + 1) * 128, 896:1152], w=[("sng", ms)])

            def nq_keys(ms):
                return [("nqb", ms)] + [("nqbp", ms, h) for h in range(4)]

            def bidx_of(m, h):
                if m in edge_ms:
                    return 20 + (edge_ms.index(m) * 4 + h) * 4
                return h * 5

            def na_X1(kp):
                m, pr = kp // 2, kp % 2
                ms = m % NQ
                hs = kp % 2
                es = kp % 3
                tiles = tiles_of(m)
                ntk = len(tiles)
                bA, bB, bC = 3 * hs, 3 * hs + 1, 3 * hs + 2
                for ti, t in enumerate(tiles):
                    bk = (bA, bA, bB, bB, bC)[ti]
                    reg = banks[bk][:, (ti % 2) * 256:(ti % 2) * 256 + 256] if ti < 4 else banks[bC][:, 0:256]
                    last_in_bank = (ti in (1, 3)) or (ti == ntk - 1)
                    S.op("pe", "matmul", dict(out=reg, lhsT=nkr2[:, t % RING, pr, :], rhs=nqb[:, ms, pr, :],
                                              start=True, stop=True),
                         r=[("nkr", t % RING)] + nq_keys(ms), w=[("ps", bk)], inc=last_in_bank)
                S.op("pe", "matmul", dict(out=banks[bC][:, 256:512], lhsT=nkm2[:, pr, :], rhs=nqb[:, ms, pr, :],
                                          start=True, stop=True),
                     r=["nkm2", "nkm2p"] + nq_keys(ms), w=[("ps", bC)])

            def na_X2(kp):
                m, pr = kp // 2, kp % 2
                ms = m % NQ
                hs = kp % 2
                es = kp % 3
                tiles = tiles_of(m)
                ntk = len(tiles)
                bA, bB, bC = 3 * hs, 3 * hs + 1, 3 * hs + 2
                S.op("act", "activation", dict(out=Et[:, es, 0:2, :], in_=banks[bA][:, 0:512].rearrange("p (a b) -> p a b", a=2),
                                               func=AF.Exp, scale=0.125), r=[("ps", bA)], w=[("Et", es)])
                n2 = min(ntk, 4) - 2
                S.op("act", "activation", dict(out=Et[:, es, 2:2 + n2, :],
                                               in_=banks[bB][:, 0:256 * n2].rearrange("p (a b) -> p a b", a=n2),
                                               func=AF.Exp, scale=0.125), r=[("ps", bB)], w=[("Et", es)])
                if ntk == 5:
                    S.op("act", "activation", dict(out=Et[:, es, 4:6, :], in_=banks[bC][:, 0:512].rearrange("p (a b) -> p a b", a=2),
                                                   func=AF.Exp, scale=0.125), r=[("ps", bC)], w=[("Et", es)])
                else:
                    S.op("act", "activation", dict(out=Et[:, es, 5, :], in_=banks[bC][:, 256:512],
                                                   func=AF.Exp, scale=0.125), r=[("ps", bC)], w=[("Et", es)])

            def na_X3(kp):
                m, pr = kp // 2, kp % 2
                ms = m % NQ
                hs = kp % 2
                es = kp % 3
                tiles = tiles_of(m)
                ntk = len(tiles)
                bA, bB, bC = 3 * hs, 3 * hs + 1, 3 * hs + 2
                for e in range(2):
                    h = 2 * pr + e
                    b0 = bidx_of(m, h)
                    meng = "dve"
                    S.op(meng, "tensor_tensor", dict(out=Etm[:, es, 0:ntk, 128 * e:128 * e + 128],
                                                     in0=Et[:, es, 0:ntk, 128 * e:128 * e + 128],
                                                     in1=Bm[:, b0:b0 + ntk, :], op=ALU.mult),
                         r=[("Et", es)] + Bm_keys, w=[("Etm", es, e)])

            def na_Y(kp):
                m, pr = kp // 2, kp % 2
                es = kp % 3
                oab = 6 + m % 2
                tiles = tiles_of(m)
                for e in range(2):
                    h = 2 * pr + e
                    oreg = banks[oab][:, h * 65:(h + 1) * 65]
                    for ti, t in enumerate(tiles):
                        S.op("pe", "matmul", dict(out=oreg, lhsT=Etm[:, es, ti, 128 * e:128 * e + 128], rhs=var_[:, t % RING, h, :],
                                                  start=(ti == 0), stop=False),
                             r=[("Etm", es, e), ("var", t % RING)], w=[("ps", oab)], inc=False)
                    S.op("pe", "matmul", dict(out=oreg, lhsT=Et[:16, es, 5, 128 * e:128 * e + 128], rhs=vma[:16, h, :],
                                              start=False, stop=True),
                         r=[("Et", es), "vma"], w=[("ps", oab)])
                if pr == 1:
                    na_Z(m)

            def na_Z(m):
                ms = m % NQ
                m2 = m % 2
                oab = 6 + m2
                oav = banks[oab][:, 0:260].rearrange("p (h e) -> p h e", h=4)
                S.op("dve", "reciprocal", dict(out=rec[:, m2, :, :], in_=oav[:, :, 64:65]), r=[("ps", oab)], w=[("rec", m2)])
                onv = ona[:, m2, :].rearrange("p (h e) -> p h e", h=4)
                S.op("dve", "tensor_tensor", dict(out=onv, in0=oav[:, :, 0:64], in1=rec[:, m2, :, :].to_broadcast([128, 4, 64]),
                                                  op=ALU.mult),
                     r=[("ps", oab), ("rec", m2)], w=[("ona", m2, 9)])
                S.op("dve", "tensor_tensor", dict(out=ona[:, m2, :], in0=ona[:, m2, :], in1=sng[:, ms, :], op=ALU.mult),
                     r=[("ona", m2, 9), ("sng", ms)], w=[("ona", m2, hh) for hh in range(4)])

            def na_Z2(m):
                m2 = m % 2
                oab = 6 + m2
                trv = banksb[oab][:, 768:1024].rearrange("p (a b) -> p a b", a=2)
                for hf in range(2):
                    S.op("pe", "transpose", dict(out=trv[:, hf, :], in_=ona[:, m2, hf * 128:(hf + 1) * 128], identity=ident),
                         r=[("ona", m2, 2 * hf), ("ona", m2, 2 * hf + 1), ("const", 0)], w=[("ps", oab)], inc=(hf == 1))
                os_ = m % NOS
                S.op("act", "activation", dict(out=oT2[:, os_, :, :], in_=trv, func=AF.Copy), r=[("ps", oab)], w=[("oT2", os_)])
                S.dma("sp", ex_in_ap(256, 512, m * 128, 128).rearrange("(a p) t -> p a t", p=128), oT2[:, os_, :, :],
                      r=[("oT2", os_)], w=[("exin", (m * 128) // XC, "n", m)])

            def na_cc(m):
                if do_cc and (m + 1) % (XC // 128) == 0:
                    k = (m * 128) // XC
                    keys = [("exin", k, "n", mm) for mm in range(k * (XC // 128), (k + 1) * (XC // 128))]
                    if debug:
                        S.dma("pool", dbg_ex[256:512, k * XC:(k + 1) * XC], exn_in_l[k][:, :], r=keys)
                    ccn[k] = S.collective(dict(kind="AllGather", op=ALU.bypass, replica_groups=RG,
                                               ins=[exn_in_l[k].ap().opt()], outs=[exn_out_l[k].ap().opt()]), r=keys)
                    if False and k >= 2:
                        for g in (2 * (k - 2), 2 * (k - 2) + 1):
                            if g < NOT:
                                load_OT(g)

            nk_tot = ntile * 2
            for m0 in range(min(LA, ntile)):
                na_L(m0)
            zq = []
            for kp in range(-3, nk_tot):
                th = []
                if kp >= 0 and kp % 2 == 0 and kp // 2 + LA < ntile:
                    th.append(lambda m=kp // 2 + LA: na_L(m))
                if 0 <= kp + 3 < nk_tot:
                    th.append(lambda kk=kp + 3: na_X1(kk))
                if 0 <= kp + 2 < nk_tot:
                    th.append(lambda kk=kp + 2: na_X2(kk))
                if 0 <= kp + 1 < nk_tot:
                    th.append(lambda kk=kp + 1: na_X3(kk))
                if 0 <= kp < nk_tot:
                    th.append(lambda kk=kp: na_Y(kk))
                interleave(th)
                if zq and zq[0][0] <= kp:
                    mz = zq.pop(0)[1]
                    na_Z2(mz)
                    na_cc(mz)
                if kp >= 0 and kp % 2 == 1:
                    zq.append((kp + 1, kp // 2))
            while zq:
                mz = zq.pop(0)[1]
                na_Z2(mz)
                na_cc(mz)
            if not do_cc:
                S.barrier()
            na_peak = A.off

        if debug in (3, 4):
            nr = 256 if debug == 3 else 512
            for k in range(nxc):
                S.dma("pool", dbg_ex[0:256, k * XC:(k + 1) * XC], exg_in_l[k][:, :])
                if nr == 512:
                    S.dma("pool", dbg_ex[256:512, k * XC:(k + 1) * XC], exn_in_l[k][:, :])
            S.barrier()
        if do_cc:

            Wo = A.alloc([16, 256], BF16)
            wost = A.alloc([2, 4, 256], F32)
            xr = A.alloc([4, 512], F32)
            osb = A.alloc([4, 512], F32)
            for k in range(4):
                s = k % 2
                S.dma("sp", wost[:, s, :, :], wout_d[512 * k:512 * (k + 1), :].rearrange("(a p) n -> p a n", p=128),
                      w=[("wost", s)])
                S.op("dve", "tensor_copy", dict(out=Wo[:, 4 * k:4 * k + 4, :], in_=wost[:, s, :, :]),
                     r=[("wost", s)], w=[("Wo", k)])
            Wo_keys = [("Wo", k) for k in range(4)]
            tcount = 0
            for g in range(seq // 512):
                gs = g % NOT
                load_OT(g)
                for ch in range(2):
                    xs = tcount % 4
                    pb = tcount % 2
                    tcount += 1
                    S.dma("sp", xr[:, xs, :], xres_d[ch * 128:(ch + 1) * 128, g * 512:(g + 1) * 512], w=[("xr", xs)])
                    for kc in range(16):
                        S.op("pe", "matmul", dict(out=banks[pb][:, 0:512], lhsT=Wo[:, kc, ch * 128:(ch + 1) * 128],
                                                  rhs=OT[:, gs, kc, :], start=(kc == 0), stop=(kc == 15)),
                             r=[("OT", gs), ("OTb", gs)] + Wo_keys, w=[("ps", pb)], inc=(kc == 15))
                    S.op("dve", "tensor_tensor", dict(out=osb[:, xs, :], in0=banks[pb][:, 0:512], in1=xr[:, xs, :], op=ALU.add),
                         r=[("ps", pb), ("xr", xs)], w=[("osb", xs)])
                    S.dma("act", out_d[ch * 128:(ch + 1) * 128, g * 512:(g + 1) * 512], osb[:, xs, :], r=[("osb", xs)])
            S.barrier(with_cc=True)

        with nc.Block() as block:
            @block.tensor
            def _(e):
                S.emit("pe", e)

            @block.scalar
            def _(e):
                S.emit("act", e)

            @block.vector
            def _(e):
                S.emit("dve", e)

            @block.gpsimd
            def _(e):
                S.emit("pool", e)

            @block.sync
            def _(e):
                S.emit("sp", e)
        info = dict(nops={e: len(S.prog[e]) for e in S.ENGS}, nwait=S.nwait, peak=A.peak,
                    gla_peak=gla_peak, na_peak=na_peak)
    return nc, info


def _consts():
    c = np.zeros((8, 128, 128), np.float32)
    c[0] = np.eye(128)
    c[1, :64, :64] = 1.0
    c[1, 64:, 64:] = 1.0
    j = np.arange(128)[:, None]
    i = np.arange(128)[None, :]
    c[2] = np.where(j <= i, -1.0 / 16, 0.0)
    c[3] = np.where(j > i, -1.0 / 16, 0.0)
    c[4] = np.where(j <= i, 1.0, 0.0)
    c[5] = np.where(j >= i, -1.0 / 16, 0.0)
    c[6] = np.where(j < i, -1.0 / 16, 0.0)
    c[7] = np.where(j >= i, 1.0, 0.0)
    return c


def _win_cols(j):
    o_gq, o_gk, o_gv, o_rf, o_rb, o_gg = 0, 512, 1024, 2048, 2064, 2080
    o_nq, o_nk, o_nv, o_ng = 3104, 4128, 5152, 6176
    r = lambda a, n: list(range(a, a + n))
    cols = []
    cols += r(o_gq + 128 * j, 128)
    cols += r(o_gk + 128 * j, 128)
    cols += r(o_nq + 256 * j, 256)
    cols += r(o_nk + 256 * j, 256)
    cols += r(o_rf, 16) + r(o_rb, 16)
    cols += r(o_gk + 128 * j, 128)
    cols += r(o_gv + 256 * j, 256)
    cols += r(o_nv + 256 * j, 256)
    cols += r(o_gg + 256 * j, 256)
    cols += r(o_ng + 256 * j, 256)
    assert len(cols) == NCOL
    return np.array(cols)


def _bias_tiles(rpb_h4, nrows):
    W = GRID_W
    ntile = nrows // 2
    c = np.arange(W)
    cs = np.clip(c - 8, 0, W - 16)
    jp = np.arange(W)[:, None]
    cc = c[None, :]
    colvalid = (jp >= cs[None, :]) & (jp < cs[None, :] + 16)
    colidx = np.clip(jp - cc + 15, 0, 30)

    def block(h, kr, qr):
        rs = min(max(qr - 4, 0), nrows - 8)
        if not (rs <= kr <= rs + 7):
            return np.full((W, W), MASKV, np.float32)
        d = kr - qr + 7
        blk = rpb_h4[h, d][colidx]
        return np.where(colvalid, blk, np.float32(MASKV)).astype(np.float32)

    def tile(h, t, m):
        out = np.empty((128, 128), np.float32)
        for a in range(2):
            for b in range(2):
                out[a * 64:(a + 1) * 64, b * 64:(b + 1) * 64] = block(h, 2 * t + a, 2 * m + b)
        return out

    tiles = np.empty((N_BT, 128, 128), np.float32)
    for h in range(4):
        for D in range(-2, 3):
            tiles[h * 5 + D + 2] = tile(h, 3 + D, 3)
    for e, m in enumerate((0, 1, ntile - 2, ntile - 1)):
        t0 = 0 if m < 2 else ntile - 4
        for tt in range(4):
            for h in range(4):
                tiles[20 + (e * 4 + h) * 4 + tt] = tile(h, t0 + tt, m)
    return tiles


def _prep_inputs(inp):
    x = np.asarray(inp["x"], np.float32)
    w_in = np.asarray(inp["w_in"], np.float32)[0]
    w_out = np.asarray(inp["w_out"], np.float32)[0]
    consts = _consts()
    gtile = np.ascontiguousarray(np.broadcast_to(np.asarray(inp["norm_g"], np.float32)[0][None, :], (128, D_MODEL)))
    gout = np.ascontiguousarray(np.broadcast_to(np.asarray(inp["gla_out_norm_g"], np.float32)[0][None, :], (128, 256)))
    qkg = np.stack([np.tile(np.asarray(inp["q_norm_g"], np.float32)[0], 2),
                    np.tile(np.asarray(inp["k_norm_g"], np.float32)[0], 2)], axis=1)
    seq = x.shape[1]
    perm = []
    for j in range(4):
        perm += list(range(256 * j, 256 * j + 256)) + list(range(1024 + 256 * j, 1024 + 256 * j + 256))
    w_out_p = w_out
    meta = np.ascontiguousarray(np.asarray(inp["meta_tokens"], np.float32))
    wdf = np.asarray(inp["w_decay_fwd"], np.float32)[0]
    wdb = np.asarray(inp["w_decay_bwd"], np.float32)[0]
    bdf = np.asarray(inp["b_decay_fwd"], np.float32)[0]
    bdb = np.asarray(inp["b_decay_bwd"], np.float32)[0]
    rpb = np.asarray(inp["rpb"], np.float32)[0]
    mbias = np.asarray(inp["meta_bias"], np.float32)[0]
    in_maps = []
    for c in range(8):
        b, j = c // 4, c % 4
        wdec = np.zeros((17, 256), np.float32)
        wdec[:16, :128] = wdf[:, 128 * j:128 * j + 128]
        wdec[:16, 128:] = wdb[:, 128 * j:128 * j + 128]
        wdec[16, :128] = bdf[128 * j:128 * j + 128]
        wdec[16, 128:] = bdb[128 * j:128 * j + 128]
        in_maps.append({
            "x": np.ascontiguousarray(x[b]),
            "meta": meta,
            "gtile": gtile,
            "w_in": np.ascontiguousarray(w_in[:, _win_cols(j)]),
            "wdec": wdec,
            "gout": gout,
            "qkg": np.ascontiguousarray(qkg),
            "biast": _bias_tiles(rpb[4 * j:4 * j + 4], seq // GRID_W),
            "metab": np.ascontiguousarray(mbias[4 * j:4 * j + 4].T),
            "w_out": np.ascontiguousarray(w_out_p[:, 256 * j:256 * j + 256]),
            "consts": consts,
            "xres": np.ascontiguousarray(x[b, :, 256 * j:256 * j + 256].T),
        })
    return in_maps


_CACHE = {}


def kernel(**inputs):
    debug = int(inputs.pop("_debug", 0))
    seq = int(np.asarray(inputs["x"]).shape[1])
    key = (seq, debug)
    if key not in _CACHE:
        _CACHE[key] = build_program(seq, debug)
    nc, info = _CACHE[key]
    in_maps = _prep_inputs(inputs)
    res = run_bass_kernel_spmd(nc, in_maps, core_ids=list(range(8)))
    out = np.empty((2, seq, D_MODEL), np.float32)
    for c in range(8):
        b, j = c // 4, c % 4
        out[b, :, 256 * j:256 * j + 256] = np.asarray(res.results[c]["out"]).T
    if debug:
        return out, res, info
    return out
```
